# Optimizing a Trainium2 kernel written in Bass

```python
import math
import jax, jax.numpy as jnp
from jax import lax
import numpy as np

D_MODEL = 2048
BATCH = 4
SEQ = 2048
DEPTH = 2
DEC_BATCH = 2
DEC_SEQ = 4096
PAST_LEN = 128

HEAD_DIM = 128
A_Q_HEADS = 4
A_KV_HEADS = 2
A_GROUP = A_Q_HEADS // A_KV_HEADS
A_HALF_WINDOW = 128
B_Q_HEADS = 6
B_KV_HEADS = 2
B_GROUP = B_Q_HEADS // B_KV_HEADS
B_QUERY_BLOCK = 128
ROPE_THETA = 10000.0
GRID_W = 64
C_PAIRS = ((128, 1), (512, 4), (2048, 16))
N_PAIRS = 3
C_SLOTS = 2
C_KV_PER_PAIR = 1
C_GROUP = C_SLOTS // C_KV_PER_PAIR
C_Q_HEADS = N_PAIRS * C_SLOTS
C_KV_HEADS = N_PAIRS * C_KV_PER_PAIR
NUM_BUCKETS = 32
REL_MAX_DISTANCE = 2048
D_FF = 5632
FFN_RESIDUAL = 0.5
N_BRANCHES = 3
NORM_EPS = 1e-6
MASK_VALUE = -1e30
IN_SIZES = (A_Q_HEADS * HEAD_DIM, A_KV_HEADS * HEAD_DIM, A_KV_HEADS * HEAD_DIM,
            B_Q_HEADS * HEAD_DIM, B_KV_HEADS * HEAD_DIM, B_KV_HEADS * HEAD_DIM,
            C_Q_HEADS * HEAD_DIM, C_KV_HEADS * HEAD_DIM, C_KV_HEADS * HEAD_DIM)
IN_COLS = sum(IN_SIZES)
IN_SPLIT_POINTS = tuple(int(v) for v in np.cumsum(IN_SIZES)[:-1])

kernel_name = 'hybrid_gated_parallel_encoder'


def rms_norm(x, gain):
    xf = x.astype(jnp.float32)
    y = xf * lax.rsqrt(jnp.mean(xf * xf, axis=-1, keepdims=True) + NORM_EPS)
    return (y * gain.astype(jnp.float32)).astype(x.dtype)


def swiglu(h, w13, w2):
    gate, up = jnp.split(h @ w13, 2, axis=-1)
    return (jax.nn.silu(gate) * up) @ w2


def t5_bucket(rel):
    half = NUM_BUCKETS // 2
    max_exact = half // 2
    ret = jnp.where(rel > 0, half, 0)
    n = jnp.abs(rel)
    nf = jnp.maximum(n, 1).astype(jnp.float32)
    large = max_exact + (jnp.log(nf / max_exact) / math.log(REL_MAX_DISTANCE / max_exact)
                         * (half - max_exact)).astype(jnp.int32)
    large = jnp.minimum(large, half - 1)
    return ret + jnp.where(n < max_exact, n, large)


def banded_gqa(q, k, v, half_window, dilation, bias_table, sink):
    n, L, kvh, g, hd = q.shape
    blk = half_window
    nb = -(-L // blk)
    lp = nb * blk
    q = jnp.pad(q, ((0, 0), (0, lp - L), (0, 0), (0, 0), (0, 0)))
    pad_kv = ((0, 0), (blk, lp - L + blk), (0, 0), (0, 0))
    kp = jnp.pad(k, pad_kv).reshape(n, nb + 2, blk, kvh, hd)
    vp = jnp.pad(v, pad_kv).reshape(n, nb + 2, blk, kvh, hd)
    kw = jnp.concatenate([kp[:, :-2], kp[:, 1:-1], kp[:, 2:]], axis=2).astype(jnp.float32)
    vw = jnp.concatenate([vp[:, :-2], vp[:, 1:-1], vp[:, 2:]], axis=2).astype(jnp.float32)
    qc = q.reshape(n, nb, blk, kvh, g, hd).astype(jnp.float32)
    logits = jnp.einsum('ncqkgd,ncskd->nckgqs', qc, kw) / math.sqrt(hd)
    rel = jnp.arange(3 * blk)[None, :] - blk - jnp.arange(blk)[:, None]
    bias = jnp.transpose(bias_table[t5_bucket(rel * dilation)], (2, 3, 0, 1)).astype(jnp.float32)
    key_pos = jnp.arange(nb)[:, None] * blk - blk + jnp.arange(3 * blk)[None, :]
    valid = (jnp.abs(rel) <= half_window)[None] & ((key_pos >= 0) & (key_pos < L))[:, None, :]
    logits = jnp.where(valid[None, :, None, None], logits + bias, MASK_VALUE)
    m = jnp.max(logits, axis=-1)
    if sink is not None:
        sink_b = sink.astype(jnp.float32)[None, None, :, :, None]
        m = jnp.maximum(m, sink_b)
    p = jnp.exp(logits - m[..., None])
    denom = jnp.sum(p, axis=-1)
    if sink is not None:
        denom = denom + jnp.exp(sink_b - m)
    out = jnp.einsum('nckgqs,ncskd->ncqkgd', p, vw)
    denom_t = jnp.transpose(denom, (0, 1, 4, 2, 3))
    out = (out / denom_t[..., None]).reshape(n, lp, kvh, g, hd)[:, :L]
    lse = jnp.transpose(m + jnp.log(denom), (0, 1, 4, 2, 3)).reshape(n, lp, kvh, g)[:, :L]
    return out, lse


def head_rms_norm(x, gain):
    xf = x.astype(jnp.float32)
    y = xf * lax.rsqrt(jnp.mean(xf * xf, axis=-1, keepdims=True) + NORM_EPS)
    return y * gain.astype(jnp.float32)


def axial_rope_angles(s):
    rows = s // GRID_W
    row_ids = jnp.repeat(jnp.arange(rows), GRID_W).astype(jnp.float32)
    col_ids = jnp.tile(jnp.arange(GRID_W), rows).astype(jnp.float32)
    axis_dim = HEAD_DIM // 2
    inv_freq = ROPE_THETA ** (-jnp.arange(0, axis_dim, 2, dtype=jnp.float32) / axis_dim)
    ang = jnp.stack([row_ids[:, None] * inv_freq, col_ids[:, None] * inv_freq], axis=1)
    return jnp.cos(ang), jnp.sin(ang)


def apply_axial_rope(x, cos, sin):
    s = x.shape[1]
    quarter = HEAD_DIM // 4
    xr = x.reshape(x.shape[:-1] + (2, 2, quarter))
    bshape = (1, s) + (1,) * (x.ndim - 3) + (2, quarter)
    c = cos.reshape(bshape)
    sn = sin.reshape(bshape)
    x1 = xr[..., 0, :]
    x2 = xr[..., 1, :]
    rot = jnp.stack([x1 * c - x2 * sn, x2 * c + x1 * sn], axis=-2)
    return rot.reshape(x.shape)


def block_sweep_gqa(q, k, v):
    b, s, kvh, g, hd = q.shape
    nb = s // B_QUERY_BLOCK
    qb = jnp.moveaxis(q.reshape(b, nb, B_QUERY_BLOCK, kvh, g, hd), 1, 0)
    kf = k.astype(jnp.float32)
    vf = v.astype(jnp.float32)

    def one_block(qblk):
        logits = jnp.einsum('bqkgd,bskd->bkgqs', qblk.astype(jnp.float32), kf) / math.sqrt(hd)
        p = jax.nn.softmax(logits, axis=-1)
        return jnp.einsum('bkgqs,bskd->bqkgd', p, vf)

    out = lax.map(one_block, qb)
    return jnp.moveaxis(out, 0, 1).reshape(b, s, kvh, g, hd)


def _stride_split(z, r):
    b, s = z.shape[:2]
    z = z.reshape((b, s // r, r) + z.shape[2:])
    return jnp.swapaxes(z, 1, 2).reshape((b * r, s // r) + z.shape[3:])


def _stride_merge(z, b, r):
    m = z.shape[1]
    z = z.reshape((b, r, m) + z.shape[2:])
    return jnp.swapaxes(z, 1, 2).reshape((b, r * m) + z.shape[3:])


def dilated_mixture(q, k, v, bias_table):
    b = q.shape[0]
    outs, lses = [], []
    for i, (window, dilation) in enumerate(C_PAIRS):
        o, l = banded_gqa(_stride_split(q[:, :, i], dilation), _stride_split(k[:, :, i], dilation),
                          _stride_split(v[:, :, i], dilation), window // (2 * dilation), dilation,
                          bias_table[:, i], None)
        outs.append(_stride_merge(o, b, dilation))
        lses.append(_stride_merge(l, b, dilation))
    weights = jax.nn.softmax(jnp.stack(lses, axis=0), axis=0)
    return jnp.sum(weights[..., None] * jnp.stack(outs, axis=0), axis=0)


def encoder_layer(x, ffn1_norm, ffn1_w13, ffn1_w2, mix_norm, w_in, q_gain_b, k_gain_b, sink_a,
                  w_gate, b_gate, w_br_a, w_br_b, w_br_c, w_o, ffn2_norm, ffn2_w13, ffn2_w2, rel_bias):
    b, s, _ = x.shape
    dt = x.dtype
    x = x + FFN_RESIDUAL * swiglu(rms_norm(x, ffn1_norm), ffn1_w13, ffn1_w2)
    h = rms_norm(x, mix_norm)
    z = h @ w_in
    aq, ak, av, bq, bk, bv, cq, ck, cv = jnp.split(z, IN_SPLIT_POINTS, axis=-1)
    ya, _ = banded_gqa(aq.reshape(b, s, A_KV_HEADS, A_GROUP, HEAD_DIM),
                       ak.reshape(b, s, A_KV_HEADS, HEAD_DIM),
                       av.reshape(b, s, A_KV_HEADS, HEAD_DIM),
                       A_HALF_WINDOW, 1,
                       rel_bias[:, :A_Q_HEADS].reshape(NUM_BUCKETS, A_KV_HEADS, A_GROUP),
                       sink_a.reshape(A_KV_HEADS, A_GROUP))
    cos, sin = axial_rope_angles(s)
    qb = apply_axial_rope(head_rms_norm(bq.reshape(b, s, B_KV_HEADS, B_GROUP, HEAD_DIM), q_gain_b), cos, sin)
    kb = apply_axial_rope(head_rms_norm(bk.reshape(b, s, B_KV_HEADS, HEAD_DIM), k_gain_b), cos, sin)
    yb = block_sweep_gqa(qb, kb, bv.reshape(b, s, B_KV_HEADS, HEAD_DIM))
    yc = dilated_mixture(cq.reshape(b, s, N_PAIRS, C_KV_PER_PAIR, C_GROUP, HEAD_DIM),
                         ck.reshape(b, s, N_PAIRS, C_KV_PER_PAIR, HEAD_DIM),
                         cv.reshape(b, s, N_PAIRS, C_KV_PER_PAIR, HEAD_DIM),
                         rel_bias[:, A_Q_HEADS:].reshape(NUM_BUCKETS, N_PAIRS, C_KV_PER_PAIR, C_GROUP))
    gates = jax.nn.sigmoid((h @ w_gate + b_gate).astype(jnp.float32)).reshape(b, s, N_BRANCHES, D_MODEL)
    br_a = (ya.reshape(b, s, -1).astype(dt) @ w_br_a).astype(jnp.float32)
    br_b = (yb.reshape(b, s, -1).astype(dt) @ w_br_b).astype(jnp.float32)
    br_c = (yc.reshape(b, s, -1).astype(dt) @ w_br_c).astype(jnp.float32)
    merged = gates[:, :, 0] * br_a + gates[:, :, 1] * br_b + gates[:, :, 2] * br_c
    x = x + merged.astype(dt) @ w_o
    x = x + FFN_RESIDUAL * swiglu(rms_norm(x, ffn2_norm), ffn2_w13, ffn2_w2)
    return x


def encoder_trunk(x, ffn1_norm, ffn1_w13, ffn1_w2, mix_norm, w_in, q_gain_b, k_gain_b, sink_a,
                  w_gate, b_gate, w_br_a, w_br_b, w_br_c, w_o, ffn2_norm, ffn2_w13, ffn2_w2,
                  rel_bias, final_norm):
    for li in range(DEPTH):
        x = encoder_layer(x, ffn1_norm[li], ffn1_w13[li], ffn1_w2[li], mix_norm[li], w_in[li],
                          q_gain_b[li], k_gain_b[li], sink_a[li], w_gate[li], b_gate[li],
                          w_br_a[li], w_br_b[li], w_br_c[li], w_o[li],
                          ffn2_norm[li], ffn2_w13[li], ffn2_w2[li], rel_bias)
    return rms_norm(x, final_norm)


def _normal(key, shape, scale):
    return jax.random.normal(key, shape, jnp.float32) * scale


def setup_inputs(seed: int = 0) -> dict:
    key = jax.random.key(seed)
    ks = jax.random.split(key, 21)
    D, L = D_MODEL, DEPTH
    return {
        'x_prompt': _normal(ks[0], (BATCH, SEQ, D), 1.0),
        'x_sample': _normal(ks[1], (DEC_BATCH, DEC_SEQ, D), 1.0),
        'ffn1_norm': 1.0 + _normal(ks[2], (L, D), 0.01),
        'ffn1_w13': _normal(ks[3], (L, D, 2 * D_FF), D ** -0.5),
        'ffn1_w2': _normal(ks[4], (L, D_FF, D), D_FF ** -0.5),
        'mix_norm': 1.0 + _normal(ks[5], (L, D), 0.01),
        'w_in': _normal(ks[6], (L, D, IN_COLS), D ** -0.5),
        'q_gain_b': 1.0 + _normal(ks[7], (L, HEAD_DIM), 0.01),
        'k_gain_b': 1.0 + _normal(ks[8], (L, HEAD_DIM), 0.01),
        'sink_a': _normal(ks[9], (L, A_Q_HEADS), 0.5),
        'w_gate': _normal(ks[10], (L, D, N_BRANCHES * D), D ** -0.5),
        'b_gate': _normal(ks[11], (L, N_BRANCHES * D), 0.02),
        'w_br_a': _normal(ks[12], (L, A_Q_HEADS * HEAD_DIM, D), (A_Q_HEADS * HEAD_DIM) ** -0.5),
        'w_br_b': _normal(ks[13], (L, B_Q_HEADS * HEAD_DIM, D), (B_Q_HEADS * HEAD_DIM) ** -0.5),
        'w_br_c': _normal(ks[14], (L, C_SLOTS * HEAD_DIM, D), (C_SLOTS * HEAD_DIM) ** -0.5),
        'w_o': _normal(ks[15], (L, D, D), D ** -0.5),
        'ffn2_norm': 1.0 + _normal(ks[16], (L, D), 0.01),
        'ffn2_w13': _normal(ks[17], (L, D, 2 * D_FF), D ** -0.5),
        'ffn2_w2': _normal(ks[18], (L, D_FF, D), D_FF ** -0.5),
        'rel_bias': _normal(ks[19], (NUM_BUCKETS, A_Q_HEADS + C_Q_HEADS), 0.1),
        'final_norm': 1.0 + _normal(ks[20], (D,), 0.01),
    }


def reference(x_prompt, x_sample, ffn1_norm, ffn1_w13, ffn1_w2, mix_norm, w_in, q_gain_b, k_gain_b,
              sink_a, w_gate, b_gate, w_br_a, w_br_b, w_br_c, w_o, ffn2_norm, ffn2_w13, ffn2_w2,
              rel_bias, final_norm):
    y_prompt = encoder_trunk(x_prompt, ffn1_norm, ffn1_w13, ffn1_w2, mix_norm, w_in, q_gain_b, k_gain_b,
                             sink_a, w_gate, b_gate, w_br_a, w_br_b, w_br_c, w_o, ffn2_norm, ffn2_w13,
                             ffn2_w2, rel_bias, final_norm)
    y_sample = encoder_trunk(x_sample, ffn1_norm, ffn1_w13, ffn1_w2, mix_norm, w_in, q_gain_b, k_gain_b,
                             sink_a, w_gate, b_gate, w_br_a, w_br_b, w_br_c, w_o, ffn2_norm, ffn2_w13,
                             ffn2_w2, rel_bias, final_norm)
    return (y_prompt, y_sample)
```

```python
import math
from contextlib import ExitStack
import numpy as np
import concourse.bass as bass
import concourse.mybir as mybir
from concourse.bass_utils import run_bass_kernel_spmd

F32 = mybir.dt.float32
BF16 = mybir.dt.bfloat16
ALU = mybir.AluOpType
AF = mybir.ActivationFunctionType

ENGS = ['pe', 'act', 'dve', 'pool', 'sp']

L = 2
D = 2048
DFF = 5632
NT = 2048
T = 1024
NTT = NT // T
KC = 16
FC = 44
GC = 4
NG = FC // GC
CL = 102
NCST = 224
NEG = -30000.0
EPS = 1e-6
SCALE = 1.0 / math.sqrt(128.0)
OH_A = 128 * 384
OH_C = 64 * 192
OH_N = OH_A + 3 * OH_C
C_B = 12.0
DEBUG_STAGE = None
DEBUG_SKIP = set()
ROPE_ENG = 'pool'
DEBUG_RG = None


class _Op:
    __slots__ = ('fn', 'deps', 'need', 'val', 'dma', 'eng')


class DmaSlot:
    def __init__(self, sem, inc=16):
        self.sem = sem
        self.count = 0
        self.inc = inc


class Sched:
    def __init__(self, nc, stack):
        self.nc = nc
        self.stack = stack
        self.ops = {e: [] for e in ENGS}
        self.res = {}
        self.esem = {e: stack.enter_context(nc.semaphore('es_' + e)) for e in ENGS if e != 'sp'}
        self.nslots = 0

    def slot(self, inc=16):
        self.nslots += 1
        return DmaSlot(self.stack.enter_context(self.nc.semaphore('ds%d' % self.nslots)), inc)

    def _deps(self, reads, writes):
        deps = []
        for r in reads:
            st = self.res.get(r)
            if st and st[0] is not None:
                deps.append(st[0])
        for w in writes:
            st = self.res.get(w)
            if st:
                if st[0] is not None:
                    deps.append(st[0])
                deps.extend(st[1].values())
                deps.extend(st[2])
        return deps

    def _update(self, tok, eng, reads, writes, is_dma):
        for r in reads:
            st = self.res.setdefault(r, [None, {}, []])
            if is_dma:
                st[2].append(tok)
            else:
                st[1][eng] = tok
        for w in writes:
            self.res[w] = [tok, {}, []]

    def op(self, eng, fn, reads=(), writes=(), extra=()):
        o = _Op()
        o.fn = fn
        o.eng = eng
        o.deps = self._deps(reads, writes) + list(extra)
        o.need = False
        o.val = None
        o.dma = None
        self.ops[eng].append(o)
        tok = ('e', o)
        self._update(tok, eng, reads, writes, False)
        return tok

    def dma(self, queue, slot, fn, reads=(), writes=(), extra=()):
        o = _Op()
        o.fn = fn
        o.eng = queue
        o.deps = self._deps(reads, writes) + list(extra)
        o.need = False
        slot.count += slot.inc
        o.dma = (slot.sem, slot.count, slot.inc)
        o.val = None
        self.ops[queue].append(o)
        tok = ('d', slot.sem, slot.count)
        self._update(tok, queue, reads, writes, True)
        return tok

    def join(self, slot, names):
        tok = ('d', slot.sem, slot.count)
        for n in names:
            self.res[n] = [tok, {}, []]

    def barrier(self):
        toks = []
        for e in ENGS:
            lst = self.ops[e]
            for o in reversed(lst):
                if o.dma is None and o.fn is not None:
                    toks.append(('e', o))
                    break
            last = {}
            for o in lst:
                if o.dma is not None:
                    last[id(o.dma[0])] = ('d', o.dma[0], o.dma[1])
            toks.extend(last.values())
        for e in ENGS:
            self.op(e, None, extra=toks)
        self.res = {}

    def finish(self, block):
        for e in ENGS:
            for o in self.ops[e]:
                for d in o.deps:
                    if d[0] == 'e' and not (d[1].eng == 'pe' and e == 'pe'):
                        d[1].need = True
        for e in ENGS:
            c = 0
            for o in self.ops[e]:
                if o.dma is None and o.need:
                    c += 1
                    o.val = c
        sched = self

        def replay(ename, h):
            seen = {}
            for o in sched.ops[ename]:
                waits = {}
                for d in o.deps:
                    if d[0] == 'e':
                        od = d[1]
                        if od.eng == 'pe' and ename == 'pe':
                            continue
                        sem, val = sched.esem[od.eng], od.val
                    else:
                        sem, val = d[1], d[2]
                    k = id(sem)
                    if seen.get(k, 0) >= val:
                        continue
                    if k not in waits or waits[k][1] < val:
                        waits[k] = (sem, val)
                for k, (sem, val) in waits.items():
                    h.wait_ge(sem, val)
                    seen[k] = val
                if o.fn is None:
                    continue
                ins = o.fn(h)
                if o.dma is not None:
                    ins.then_inc(o.dma[0], o.dma[2])
                elif o.need:
                    ins.then_inc(sched.esem[ename], 1)

        @block.tensor
        def _(h):
            replay('pe', h)

        @block.scalar
        def _(h):
            replay('act', h)

        @block.vector
        def _(h):
            replay('dve', h)

        @block.gpsimd
        def _(h):
            replay('pool', h)

        @block.sync
        def _(h):
            replay('sp', h)


def build():
    nc = bass.Bass("TRN2", target_bir_lowering=False)
    dbg = DEBUG_STAGE is not None

    def din(name, shape, dt=F32):
        return nc.dram_tensor(name, shape, dt, kind="ExternalInput")

    x_d = din("x", [NT, D])
    w13_d = [din("w13_1", [L * FC, 128, 4096]), din("w13_2", [L * FC, 128, 4096])]
    w2_d = [din("w2_1", [L * FC, 128, D]), din("w2_2", [L * FC, 128, D])]
    wqk_d = din("wqk", [L * 23, 128, 2048])
    wv_d = din("wv", [L, 128, 16 * 896])
    wg_d = din("wg", [L * 48, 128, 2048])
    wbr_d = din("wbr", [L * 16, 128, 1536])
    wo_d = din("wo", [L * 16, 128, 2048])
    cst_d = din("cst", [128, NCST])
    ropec_d = din("rope_c", [128, NT])
    ropes_d = din("rope_s", [128, NT])
    oh_d = din("oh", [33, OH_N])
    rbx_d = din("rbx", [33, 10])
    perm_d = din("perm", [128, 128])
    ident_d = din("ident", [128, 128])
    y_d = nc.dram_tensor("y", [NT, D], F32, kind="ExternalOutput")

    dk = dict(kind="ExternalOutput") if dbg else {}
    xT_s = nc.dram_tensor("xT_s", [KC, 128, NT], F32, **dk)
    hn_s = nc.dram_tensor("hn_s", [KC, 128, NT], BF16, **dk)
    q_s = nc.dram_tensor("q_s", [16, 128, NT], BF16, **dk)
    kT_loc = nc.dram_tensor("kT_loc", [7 * 128, NT], BF16)
    kT_all = nc.dram_tensor("kT_all", [2 * 7 * 128, NT], BF16)
    v_loc = nc.dram_tensor("v_loc", [NT, 896], BF16)
    v_all = nc.dram_tensor("v_all", [2 * NT, 896], BF16)
    y_s = nc.dram_tensor("y_s", [12, 128, NT], BF16, **dk)
    bm_s = nc.dram_tensor("bm_s", [10, OH_N], F32, **dk)

    with ExitStack() as st:
        S = Sched(nc, st)

        def sb(name, shape, dt):
            return st.enter_context(nc.sbuf_tensor("sb_" + name, shape, dt))

        xT = sb("xT", [128, KC, T], F32)
        xn = sb("xn", [128, KC, T], BF16)
        AR = sb("AR", [128, 22528], F32)
        cst = sb("cst", [128, NCST], F32)
        onesD = sb("onesD", [128, 128], F32)
        onesH = sb("onesH", [128, 128], F32)
        ident = sb("ident", [128, 128], F32)
        perm = sb("perm", [128, 128], F32)
        ones_bf = sb("ones_bf", [128, 128], BF16)
        onesD_bf = sb("onesD_bf", [128, 128], BF16)
        epsb = sb("epsb", [128, 1], F32)
        negcA = sb("negcA", [128, 1], F32)
        negcB = sb("negcB", [128, 2], F32)
        sinkexp = sb("sinkexp", [128, 4], F32)
        rstd = sb("rstd", [128, T], F32)
        sq = [sb("sq%d" % i, [128, 512], F32) for i in range(2)]
        sg = [sb("sg%d" % i, [128, 512], F32) for i in range(2)]
        P = [st.enter_context(nc.psum_tensor("P%d" % i, [128, 512], F32)) for i in range(8)]
        block = st.enter_context(nc.Block())

        xT_flat = xT[:, :, :].rearrange("p k n -> p (k n)")
        xn_flat = xn[:, :, :].rearrange("p k n -> p (k n)")

        def carve(base, off_bytes, nbytes, dt):
            assert off_bytes % 4 == 0 and nbytes % 4 == 0
            v = base[:, off_bytes // 4:(off_bytes + nbytes) // 4]
            if dt == BF16:
                v = v.bitcast(BF16)
            return v

        def carve_bf(base_bf, off_el, n_el):
            return base_bf[:, off_el:off_el + n_el]

        def mm(out, lhsT, rhs, start, stop, reads, writes):
            S.op('pe', lambda h: h.matmul(out, lhsT=lhsT, rhs=rhs, start=start, stop=stop), reads, writes)

        def act(out, in_, func, reads, writes, bias=None, scale=None):
            kw = {}
            if bias is not None:
                kw['bias'] = bias
            if scale is not None:
                kw['scale'] = scale
            return S.op('act', lambda h: h.activation(out=out, in_=in_, func=func, **kw), reads, writes)

        def dve_tt(out, in0, in1, op, reads, writes, eng='dve'):
            S.op(eng, lambda h: h.tensor_tensor(out=out, in0=in0, in1=in1, op=op), reads, writes)

        def dve_stt(out, in0, scalar, in1, op0, op1, reads, writes, eng='dve'):
            S.op(eng, lambda h: h.scalar_tensor_tensor(out=out, in0=in0, scalar=scalar, in1=in1, op0=op0, op1=op1),
                 reads, writes)

        def dve_copy(out, in_, reads, writes, eng='dve'):
            S.op(eng, lambda h: h.tensor_copy(out=out, in_=in_), reads, writes)

        def dve_recip(out, in_, reads, writes):
            S.op('dve', lambda h: h.reciprocal(out=out, in_=in_), reads, writes)

        def dma(queue, slot, out, in_, reads, writes):
            return S.dma(queue, slot, lambda h: h.dma_start(out=out, in_=in_), reads, writes)

        def memset(ap, val, writes, eng='dve'):
            S.op(eng, lambda h: h.memset(ap, val), (), writes)

        slots = {}

        def SL(name):
            if name not in slots:
                slots[name] = S.slot()
            return slots[name]

        dma('sp', SL('c0'), cst[:, :], cst_d[:, :], [], ['cst'])
        dma('sp', SL('c1'), ident[:, :], ident_d[:, :], [], ['ident'])
        dma('sp', SL('c2'), perm[:, :], perm_d[:, :], [], ['perm'])
        memset(onesD[:, :], 1.0 / D, ['onesD'])
        memset(onesH[:, :], 1.0 / 128, ['onesH'])
        memset(ones_bf[:, :], 1.0, ['ones_bf'])
        memset(onesD_bf[:, :], 1.0 / D, ['onesD_bf'])
        memset(epsb[:, :], EPS, ['epsb'])
        memset(negcA[:, :], 0.0, ['negcA'])
        S.op('dve', lambda h: h.tensor_scalar(out=negcB[:, 0:2], in0=cst[:, 220:222], scalar1=-C_B, scalar2=None,
                                              op0=ALU.add), ['cst'], ['negcB'])

        def bm_compute():
            rbx = carve(AR, 0, 64, F32)[0:33, 0:10]
            dma('sp', SL('c3'), rbx, rbx_d[:, :], [], ['rbx'])
            PIECE = 4096
            npieces = OH_N // PIECE
            for pi in range(npieces):
                s2 = pi % 2
                ohb = carve(AR, 1024 + s2 * PIECE * 4, PIECE * 4, F32)
                stg = carve(AR, 1024 + 2 * PIECE * 4 + s2 * PIECE * 4, PIECE * 4, F32)
                dma('sp', SL('oh%d' % s2), ohb[0:33, :], oh_d[:, pi * PIECE:(pi + 1) * PIECE], [], [('oh', s2)])
                for q in range(PIECE // 512):
                    b = q % 4
                    mm(P[b][0:10, :], rbx, ohb[0:33, q * 512:(q + 1) * 512], True, True,
                       ['rbx', ('oh', s2)], [('P', b)])
                    if q % 2 == 0:
                        dve_copy(stg[0:10, q * 512:(q + 1) * 512], P[b][0:10, :], [('P', b)], [('stg', s2, q)])
                    else:
                        act(stg[0:10, q * 512:(q + 1) * 512], P[b][0:10, :], AF.Copy, [('P', b)], [('stg', s2, q)])
                dma('sp', SL('bmst%d' % s2), bm_s[:, pi * PIECE:(pi + 1) * PIECE], stg[0:10, :],
                    [('stg', s2, q) for q in range(PIECE // 512)], [('bm_s', pi)])

        S.barrier()

        def load_x_tile(tt):
            for tb in range(T // 128):
                s2 = tb % 2
                xin = carve(AR, s2 * 8192, 8192, F32)
                r0 = tt * T + tb * 128
                dma('sp', SL('xin%d' % s2), xin, x_d[r0:r0 + 128, :], [], [('xin', s2)])
                for k4 in range(4):
                    b = (tb * 4 + k4) % 4
                    for kk in range(4):
                        k = k4 * 4 + kk
                        mm(P[b][:, kk * 128:(kk + 1) * 128], xin[:, k * 128:(k + 1) * 128], ident[:, :], True, True,
                           [('xin', s2), 'ident'], [('P', b)])
                    outv = xT[:, k4 * 4:(k4 + 1) * 4, tb * 128:(tb + 1) * 128]
                    inv = P[b][:, :].rearrange("p (a b) -> p a b", a=4)
                    wr = [('xT', k4 * 4 + kk, tb // 4) for kk in range(4)]
                    if k4 % 2 == 0:
                        dve_copy(outv, inv, [('P', b)], wr)
                    else:
                        act(outv, inv, AF.Copy, [('P', b)], wr)

        def norm_stats():
            for half in range(2):
                for k in range(KC):
                    s2 = k % 2
                    sqb = sq[s2][:, 0:256].bitcast(BF16)
                    act(sqb, xT[:, k, half * 512:(half + 1) * 512], AF.Square,
                        [('xT', k, half)], [('sq', s2)])
                    mm(P[7][:, :], onesD_bf[:, :], sqb, k == 0, k == KC - 1,
                       [('sq', s2), 'onesD_bf'], [('P', 7)])
                act(rstd[:, half * 512:(half + 1) * 512], P[7][:, :], AF.Sqrt, [('P', 7), 'epsb'], [('rstd', half)],
                    bias=epsb[:, 0:1], scale=1.0)
                dve_recip(P[5 + half][:, :], rstd[:, half * 512:(half + 1) * 512],
                          [('rstd', half)], [('P', 5 + half)])

        def norm_to_xn(gcol):
            norm_stats()
            for k in range(KC):
                for half in range(2):
                    hs = slice(half * 512, (half + 1) * 512)
                    dve_stt(xn[:, k, hs], xT[:, k, hs], cst[:, gcol + k:gcol + k + 1], P[5 + half][:, :],
                            ALU.mult, ALU.mult, [('xT', k, half), ('P', 5 + half), 'cst'], [('xn', k, half)])

        def ffn(l, which):
            w13b = [carve(AR, i * 8192, 8192, BF16) for i in range(4)]
            w2b = [carve(AR, 32768 + i * 16384, 16384, BF16) for i in range(2)]
            hTb = [carve(AR, 65536 + i * 8192, 8192, BF16) for i in range(2)]
            norm_to_xn(l * CL + (0 if which == 0 else 32))

            def second(g):
                gs = g % 2
                for m in range(KC):
                    for half in range(2):
                        b = 4 + (m * 2 + half) % 3
                        for ci in range(GC):
                            mm(P[b][:, :], w2b[gs][:, ci * 2048 + m * 128:ci * 2048 + (m + 1) * 128],
                               hTb[gs][:, ci * 1024 + half * 512:ci * 1024 + (half + 1) * 512],
                               ci == 0, ci == GC - 1, [('w2', gs), ('hT', gs, ci, half)], [('P', b)])
                        xs = xT[:, m, half * 512:(half + 1) * 512]
                        dve_stt(xs, P[b][:, :], 0.5, xs, ALU.mult, ALU.add, [('P', b), ('xT', m, half)],
                                [('xT', m, half)])

            pending = None
            for g in range(NG):
                gs = g % 2
                for ci in range(GC):
                    c = g * GC + ci
                    s4 = c % 4
                    dma('pool', SL('w13_%d' % s4), w13b[s4].rearrange("p (a b) -> p a b", a=2),
                        w13_d[which][l * FC + c].rearrange("p (a b) -> p a b", a=2), [], [('w13', s4)])
                    for half in range(2):
                        u = (c * 2 + half) % 2
                        for gu in range(2):
                            b = 2 * u + gu
                            for k in range(KC):
                                mm(P[b][:, :], w13b[s4][:, k * 256 + gu * 128:k * 256 + (gu + 1) * 128],
                                   xn[:, k, half * 512:(half + 1) * 512], k == 0, k == KC - 1,
                                   [('w13', s4), ('xn', k, half)], [('P', b)])
                        act(sg[u][:, :], P[2 * u][:, :], AF.Silu, [('P', 2 * u)], [('sg', u)])
                        dve_tt(hTb[gs][:, ci * 1024 + half * 512:ci * 1024 + (half + 1) * 512], sg[u][:, :],
                               P[2 * u + 1][:, :], ALU.mult, [('sg', u), ('P', 2 * u + 1)], [('hT', gs, ci, half)])
                dma('pool', SL('w2_%d' % gs), w2b[gs].rearrange("p (c n) -> p c n", c=GC),
                    w2_d[which][l * FC + g * GC:l * FC + (g + 1) * GC].rearrange("c p n -> p c n"), [],
                    [('w2', gs)])
                if pending is not None:
                    second(pending)
                pending = g
            second(pending)

        QK_DEST = ([('q', i) for i in range(4)] + [('k', 0), ('k', 1)] + [('q', 4 + i) for i in range(6)] +
                   [('k', 2), ('k', 3)] + [('q', 10 + i) for i in range(6)] + [('k', 4), ('k', 5), ('k', 6)])

        def mix_in(l, tt):
            t0 = tt * T
            wqb = [carve(AR, 69632 + i * 4096, 4096, BF16) for i in range(3)]
            wvb = carve(AR, 8192, 28672, BF16)
            zq = [carve(AR, 36864 + i * 2048, 2048, F32) for i in range(2)]
            sq2 = [carve(AR, 40960 + i * 2048, 2048, F32) for i in range(2)]
            rs2 = [carve(AR, 45056 + i * 2048, 2048, F32) for i in range(2)]
            qn = [carve(AR, 49152 + i * 2048, 2048, F32) for i in range(2)]
            r1 = [carve(AR, 53248 + i * 2048, 2048, F32) for i in range(2)]
            r2 = [carve(AR, 57344 + i * 2048, 2048, F32) for i in range(2)]
            zst = [carve(AR, 61440 + i * 1024, 1024, BF16) for i in range(4)]
            vst = [carve(AR, 65536 + i * 1792, 1792, BF16) for i in range(2)]
            ropec = carve(AR, 0, 4096, F32)
            ropes = carve(AR, 4096, 4096, F32)
            if 'm' in DEBUG_SKIP:
                memset(ropec, 0.0, ['ropec'])
                memset(ropes, 0.0, ['ropes'])

            norm_to_xn(l * CL + 16)
            if 'a' not in DEBUG_SKIP:
                dma('sp', SL('xTst'), xT_s[:, :, t0:t0 + T].rearrange("k p t -> p k t"), xT[:, :, :],
                    [('xT', k, h_) for k in range(KC) for h_ in range(2)], [('xT_s', tt)])
            if 'b' not in DEBUG_SKIP:
                dma('sp', SL('hnst'), hn_s[:, :, t0:t0 + T].rearrange("k p t -> p k t"), xn[:, :, :],
                    [('xn', k, h_) for k in range(KC) for h_ in range(2)], [('hn_s', tt)])
            dma('sp', SL('ropec'), ropec, ropec_d[:, t0:t0 + T], [], ['ropec'])
            dma('sp', SL('ropes'), ropes, ropes_d[:, t0:t0 + T], [], ['ropes'])
            dma('pool', SL('wv'), wvb.rearrange("p (k n) -> p k n", k=KC),
                wv_d[l].rearrange("p (k n) -> p k n", k=KC), [], ['wv'])
            ui = 0

            def issue_qk(j):
                dma('pool', SL('wqk%d' % (j % 3)), wqb[j % 3], wqk_d[l * 23 + j], [], [('wqk', j % 3)])

            issue_qk(0)
            issue_qk(1)
            pend = []
            kst = []
            for j in range(23):
                s2 = j % 3
                if j + 2 < 23:
                    issue_qk(j + 2)
                kind, idx = QK_DEST[j]
                isB = (6 <= j <= 13) and 'h' not in DEBUG_SKIP
                for half in range(2):
                    b = ui % 3
                    z4 = ui % 4
                    r_ = ui % 2
                    ui += 1
                    for k in range(KC):
                        mm(P[b][:, :], wqb[s2][:, k * 128:(k + 1) * 128], xn[:, k, half * 512:(half + 1) * 512],
                           k == 0, k == KC - 1, [('wqk', s2), ('xn', k, half)], [('P', b)])
                    if not isB:
                        act(zst[z4], P[b][:, :], AF.Copy, [('P', b)], [('zst', z4)])
                    else:
                        gcol = l * CL + 96 + (0 if j <= 11 else 1)
                        act(zq[r_], P[b][:, :], AF.Copy, [('P', b), 'cst'], [('zq', r_)], scale=cst[:, gcol:gcol + 1])
                        act(sq2[r_], P[b][:, :], AF.Square, [('P', b)], [('sq2', r_)])
                        mm(P[3][:, :], onesH[:, :], sq2[r_], True, True, [('sq2', r_), 'onesH'], [('P', 3)])
                        act(rs2[r_], P[3][:, :], AF.Sqrt, [('P', 3), 'epsb'], [('rs2', r_)], bias=epsb[:, 0:1],
                            scale=1.0)
                        dve_recip(rs2[r_], rs2[r_], [('rs2', r_)], [('rs2', r_)])
                        dve_tt(qn[r_], zq[r_], rs2[r_], ALU.mult, [('zq', r_), ('rs2', r_)], [('qn', r_)], eng=ROPE_ENG)

                        def tail(r_=r_, z4=z4, half=half, kind=kind, idx=idx):
                            mm(P[4][:, :], perm[:, :], qn[r_], True, True, [('qn', r_), 'perm'], [('P', 4)])
                            dve_tt(r1[r_], qn[r_], ropec[:, half * 512:(half + 1) * 512], ALU.mult,
                                   [('qn', r_), 'ropec'], [('r1', r_)], eng=ROPE_ENG)
                            dve_tt(r2[r_], P[4][:, :], ropes[:, half * 512:(half + 1) * 512], ALU.mult,
                                   [('P', 4), 'ropes'], [('r2', r_)])
                            dve_tt(zst[z4], r1[r_], r2[r_], ALU.add, [('r1', r_), ('r2', r_)], [('zst', z4)],
                                   eng=ROPE_ENG)
                            c0_ = t0 + half * 512
                            if kind == 'q':
                                dst_ = q_s[idx, :, c0_:c0_ + 512]
                            else:
                                dst_ = kT_loc[idx * 128:(idx + 1) * 128, c0_:c0_ + 512]
                            tk_ = dma('sp', SL('zst%d' % z4), dst_, zst[z4], [('zst', z4)], [(kind, idx, tt, half)])
                            if kind == 'k':
                                kst.append(tk_)

                        pend.append(tail)
                        if len(pend) > 1:
                            pend.pop(0)()
                        continue
                    while pend:
                        pend.pop(0)()
                    c0 = t0 + half * 512
                    if kind == 'q':
                        dst = q_s[idx, :, c0:c0 + 512]
                    else:
                        dst = kT_loc[idx * 128:(idx + 1) * 128, c0:c0 + 512]
                    tk_ = dma('sp', SL('zst%d' % z4), dst, zst[z4], [('zst', z4)], [(kind, idx, tt, half)])
                    if kind == 'k':
                        kst.append(tk_)
            while pend:
                pend.pop(0)()
            if tt == NTT - 1:
                exchange_k(kst)
            for tb in range(T // 128):
                if 'e' in DEBUG_SKIP:
                    break
                s2 = tb % 2
                for (n0, n1, b) in ((0, 512, 2 * s2), (512, 896, 2 * s2 + 1)):
                    for k in range(KC):
                        mm(P[b][:, 0:n1 - n0], xn[:, k, tb * 128:(tb + 1) * 128],
                           wvb[:, k * 896 + n0:k * 896 + n1], k == 0, k == KC - 1, ['wv', ('xn', k, tb // 4)], [('P', b)])
                act(vst[s2][:, 0:512], P[2 * s2][:, :], AF.Copy, [('P', 2 * s2)], [('vst', s2, 0)])
                dve_copy(vst[s2][:, 512:896], P[2 * s2 + 1][:, 0:384], [('P', 2 * s2 + 1)], [('vst', s2, 1)])
                r0 = t0 + tb * 128
                dma('sp', SL('vst%d' % s2), v_loc[r0:r0 + 128, :], vst[s2], [('vst', s2, 0), ('vst', s2, 1)],
                    [('v_loc', tt, tb)])

        cc_slot = S.slot(inc=1)

        cc_state = {'t1': None}

        def exchange_k(extra):
            RG = DEBUG_RG or [[0, 1], [2, 3], [4, 5], [6, 7]]
            t1 = None
            for ch in range(7):
                t1 = S.dma('pool', cc_slot, lambda h, ch=ch: h.collective_compute(
                    "AllGather", ALU.bypass, replica_groups=RG, ins=[kT_loc[ch * 128:(ch + 1) * 128, :].opt()],
                    outs=[kT_all[ch * 256:(ch + 1) * 256, :].opt()]), extra=([t1] if t1 else list(extra)))
            cc_state['t1'] = t1

        def exchange():
            RG = DEBUG_RG or [[0, 1], [2, 3], [4, 5], [6, 7]]
            t1 = cc_state['t1']
            for vc in range(8):
                t1 = S.dma('pool', cc_slot, lambda h, vc=vc: h.collective_compute(
                    "AllGather", ALU.bypass, replica_groups=RG, ins=[v_loc[vc * 256:(vc + 1) * 256, :].opt()],
                    outs=[v_all[vc * 512:(vc + 1) * 512, :].opt()]), extra=([t1] if t1 else []))
            cc_state['t1'] = None

        KALL = kT_all.ap().rearrange("(c s p) n -> s p c n", c=7, s=2, p=128)
        VALL = v_all.ap().rearrange("(c s i) d -> s c i d", c=8, s=2, i=256)

        xn_bf = xn_flat
        AR_bf = AR[:, :].bitcast(BF16)
        xT_bf = xT_flat.bitcast(BF16)

        class Batch:
            def __init__(self, name):
                self.slot = SL(name)
                self.names = []

            def ld(self, out, in_, name):
                q = 'sp' if len(self.names) % 2 == 0 else 'act'
                dma(q, self.slot, out, in_, [], [name])
                self.names.append(name)

            def done(self):
                S.join(self.slot, self.names)

        def attn_A(l):
            qA = xn_bf[:, 0:4 * NT].rearrange("p (h n) -> p h n", h=4)
            kA = AR_bf[:, 0:2 * 2304].rearrange("p (h n) -> p h n", h=2)
            vA = AR_bf[:, 4608:4608 + 18 * 256].rearrange("p (b c) -> p b c", b=18)
            yA = AR_bf[:, 9216:9216 + 4 * NT].rearrange("p (h n) -> p h n", h=4)
            BM = xT_flat[:, 0:4 * 3 * 384].rearrange("p (h v n) -> p h v n", h=4, v=3)
            lg = [xT_flat[:, 4608 + i * 384:4608 + (i + 1) * 384] for i in range(2)]
            pT = [xT_bf[:, 2 * 5376 + i * 384:2 * 5376 + (i + 1) * 384] for i in range(3)]
            dd = [xT_flat[:, 6144 + i * 128:6144 + (i + 1) * 128] for i in range(2)]
            B_ = Batch('ldA')
            B_.ld(qA, q_s[0:4].rearrange("h p n -> p h n"), 'qA')
            B_.ld(kA[:, :, 128:128 + NT], kT_loc[0:256, :].rearrange("(h p) n -> p h n", p=128), 'kA0')
            B_.ld(kA[:, :, 0:128], KALL[0][:, 0:2, NT - 128:NT], 'kA1')
            B_.ld(kA[:, :, 128 + NT:256 + NT], KALL[1][:, 0:2, 0:128], 'kA2')
            B_.ld(vA[:, 1:17, :], v_loc[:, 0:256].rearrange("(b p) c -> p b c", p=128), 'vA0')
            B_.ld(vA[:, 0, :], VALL[0][7, 128:256, 0:256], 'vA1')
            B_.ld(vA[:, 17, :], VALL[1][0, 0:128, 0:256], 'vA2')
            for h in range(4):
                for v in range(3):
                    B_.ld(BM[:, h, v, :], bm_s[h, 0:OH_A].rearrange("(k n) -> k n", k=128), ('BM', h, v))
            B_.done()
            for h in range(4):
                S.op('dve', lambda h_, h=h: h_.tensor_scalar(out=BM[:, h, 1, 0:128], in0=BM[:, h, 1, 0:128],
                                                            scalar1=cst[:, 222:223], scalar2=None, op0=ALU.add),
                     [('BM', h, 1), 'cst'], [('BM', h, 1)])
                S.op('dve', lambda h_, h=h: h_.tensor_scalar(out=BM[:, h, 2, 256:384], in0=BM[:, h, 2, 256:384],
                                                            scalar1=cst[:, 223:224], scalar2=None, op0=ALU.add),
                     [('BM', h, 2), 'cst'], [('BM', h, 2)])
            act(sinkexp[:, 0:4], cst[:, l * CL + 98:l * CL + 102], AF.Exp, ['cst', 'negcA'], ['sinkexp'],
                bias=negcA[:, 0:1], scale=1.0)
            rk = ['qA', 'kA0', 'kA1', 'kA2']
            rv = ['vA0', 'vA1', 'vA2']
            units = [(h, i) for h in range(4) for i in range(16)]

            def score(ui):
                h, i = units[ui]
                kv = h // 2
                var = 1 if i == 0 else (2 if i == 15 else 0)
                bs = ui % 3
                l2 = ui % 2
                p3 = ui % 3
                for jj in range(3):
                    mm(P[bs][:, jj * 128:(jj + 1) * 128], kA[:, kv, (i + jj) * 128:(i + jj + 1) * 128],
                       qA[:, h, i * 128:(i + 1) * 128], True, True, rk, [('P', bs)])
                dve_stt(lg[l2], P[bs][:, 0:384], SCALE, BM[:, h, var, :], ALU.mult, ALU.add,
                        [('P', bs), ('BM', h, var)], [('lg', l2)])
                act(pT[p3], lg[l2], AF.Exp, [('lg', l2), 'negcA'], [('pT', p3)], bias=negcA[:, 0:1], scale=1.0)

            def finish(ui):
                h, i = units[ui]
                kv = h // 2
                bo = 3 + ui % 3
                l2 = ui % 2
                p3 = ui % 3
                for jj in range(3):
                    mm(P[bo][:, 0:128], vA[:, i + jj, kv * 128:(kv + 1) * 128], pT[p3][:, jj * 128:(jj + 1) * 128],
                       jj == 0, jj == 2, rv + [('pT', p3)], [('P', bo)])
                for jj in range(3):
                    mm(P[bo][:, 128:256], ones_bf[:, :], pT[p3][:, jj * 128:(jj + 1) * 128],
                       jj == 0, jj == 2, ['ones_bf', ('pT', p3)], [('P', bo)])
                S.op('dve', lambda h_, o=dd[l2], i_=P[bo][:, 128:256], s_=sinkexp[:, h:h + 1]:
                     h_.tensor_scalar(out=o, in0=i_, scalar1=s_, scalar2=None, op0=ALU.add),
                     [('P', bo), 'sinkexp'], [('dd', l2)])
                dve_recip(dd[l2], dd[l2], [('dd', l2)], [('dd', l2)])
                dve_tt(yA[:, h, i * 128:(i + 1) * 128], P[bo][:, 0:128], dd[l2], ALU.mult,
                       [('P', bo), ('dd', l2)], [('yA', h)])
                if i == 15:
                    dma('sp', SL('ya'), y_s[h], yA[:, h, :], [('yA', h)], [('y_s', h)])

            score(0)
            for ui in range(len(units)):
                if ui + 1 < len(units):
                    score(ui + 1)
                finish(ui)

        def attn_B(l):
            qB = xn_bf[:, 0:6 * NT].rearrange("p (h n) -> p h n", h=6)
            kB = AR_bf[:, 0:2 * 2 * NT].rearrange("p (h n) -> p h n", h=2)
            vB = AR_bf[:, 8192:8192 + 32 * 256].rearrange("p (b c) -> p b c", b=32)
            pT = [xT_bf[:, i * 512:(i + 1) * 512] for i in range(4)]
            rd = [xT_flat[:, 2048 + i * 512:2048 + (i + 1) * 512] for i in range(2)]
            yB = [xT_bf[:, 8192 + i * 512:8192 + (i + 1) * 512] for i in range(2)]
            B_ = Batch('ldB')
            B_.ld(qB, q_s[4:10].rearrange("h p n -> p h n"), 'qB')
            for s in range(2):
                B_.ld(kB[:, :, s * NT:(s + 1) * NT], KALL[s][:, 2:4, :], ('kB', s))
                for c8 in range(8):
                    B_.ld(vB[:, s * 16 + c8 * 2:s * 16 + c8 * 2 + 2, :],
                          VALL[s][c8, :, 256:512].rearrange("(b p) d -> p b d", p=128), ('vB', s, c8))
            B_.done()
            units = [(h, qt, kb) for h in range(6) for qt in range(4) for kb in range(32)]

            def score(ui):
                h, qt, kb = units[ui]
                kv = h // 3
                seg = kb // 16
                bs = ui % 4
                mm(P[bs][:, :], kB[:, kv, kb * 128:(kb + 1) * 128], qB[:, h, qt * 512:(qt + 1) * 512],
                   True, True, ['qB', ('kB', seg)], [('P', bs)])
                act(pT[bs], P[bs][:, :], AF.Exp, [('P', bs), 'negcB'], [('pT', bs)],
                    bias=negcB[:, seg:seg + 1], scale=SCALE)

            def finish(ui):
                h, qt, kb = units[ui]
                kv = h // 3
                seg = kb // 16
                p4 = ui % 4
                ti = ui // 32
                bo = 4 + 2 * (ti % 2)
                y2 = ti % 2
                mm(P[bo][:, :], vB[:, kb, kv * 128:(kv + 1) * 128], pT[p4], kb == 0, kb == 31,
                   [('vB', seg, (kb % 16) // 2), ('pT', p4)], [('P', bo)])
                mm(P[bo + 1][:, :], ones_bf[:, :], pT[p4], kb == 0, kb == 31,
                   ['ones_bf', ('pT', p4)], [('P', bo + 1)])
                if kb == 31:
                    dve_recip(rd[y2], P[bo + 1][:, :], [('P', bo + 1)], [('rd', y2)])
                    dve_tt(yB[y2], P[bo][:, :], rd[y2], ALU.mult, [('P', bo), ('rd', y2)], [('yB', y2)])
                    dma('sp', SL('yb%d' % y2), y_s[4 + h, :, qt * 512:(qt + 1) * 512], yB[y2], [('yB', y2)],
                        [('y_s', 4 + h, qt)])

            score(0)
            score(1)
            for ui in range(len(units)):
                if ui + 2 < len(units):
                    score(ui + 2)
                finish(ui)

        def attn_C(l):
            qC = xn_bf[:, 0:6 * NT].rearrange("p (h n) -> p h n", h=6)
            ACC = AR[:, 0:4 * NT].rearrange("p (a n) -> p a n", a=4)
            RS = [1, 4, 16]
            koff = []
            vboff = []
            o = 0
            ob = 0
            for r in RS:
                koff.append(o)
                o += NT + 128 * r
                vboff.append(ob)
                ob += 32 + 2 * r
            kC = xT_bf[:, 0:o]
            vC = AR_bf[:, 16384:16384 + ob * 128].rearrange("p (b c) -> p b c", b=ob)
            BMc = xT_flat[:, 4608:4608 + 3 * 3 * 384].rearrange("p (a v n) -> p a v n", a=3, v=3)
            lg = [xT_flat[:, 8192 + i * 384:8192 + (i + 1) * 384] for i in range(2)]
            pT = [xT_bf[:, 2 * 9216 + i * 384:2 * 9216 + (i + 1) * 384] for i in range(3)]
            yC = xT_bf[:, 2 * 10240:2 * 10240 + 2 * NT].rearrange("p (g n) -> p g n", g=2)
            rdc = xT_flat[:, 12288:12288 + NT]
            B_ = Batch('ldC')
            B_.ld(qC, q_s[10:16].rearrange("h p n -> p h n"), 'qC')
            rk = ['qC']
            rv = []
            for p, r in enumerate(RS):
                hw = 64 * r
                ko = koff[p]
                krow = (4 + p) * 128
                B_.ld(kC[:, ko + hw:ko + hw + NT], kT_loc[krow:krow + 128, :], ('kC', p, 0))
                B_.ld(kC[:, ko:ko + hw], KALL[0][:, 4 + p, NT - hw:NT], ('kC', p, 1))
                B_.ld(kC[:, ko + hw + NT:ko + 2 * hw + NT], KALL[1][:, 4 + p, 0:hw], ('kC', p, 2))
                rk += [('kC', p, 0), ('kC', p, 1), ('kC', p, 2)]
                nb = 32 // r
                vc = v_loc[:, 512 + p * 128:512 + (p + 1) * 128]
                for c in range(r):
                    base = vboff[p] + c * (nb + 2)
                    src = vc.rearrange("(b i c) d -> c i b d", i=64, c=r)[c]
                    B_.ld(vC[0:64, base + 1:base + 1 + nb, :], src, ('vC', p, c, 0))
                    cs = slice(512 + p * 128, 512 + (p + 1) * 128)
                    npc = max(1, hw // 256)
                    plen = hw // npc
                    for pc in range(npc):
                        i0 = pc * plen // r
                        ni = plen // r
                        tk = NT - hw + pc * plen
                        src0 = VALL[0][tk // 256, tk % 256:tk % 256 + plen, cs].rearrange("(i c) d -> c i d", c=r)[c]
                        B_.ld(vC[i0:i0 + ni, base, :], src0, ('vC', p, c, 1, pc))
                        tk = pc * plen
                        src1 = VALL[1][tk // 256, tk % 256:tk % 256 + plen, cs].rearrange("(i c) d -> c i d", c=r)[c]
                        B_.ld(vC[i0:i0 + ni, base + 1 + nb, :], src1, ('vC', p, c, 2, pc))
                        rv += [('vC', p, c, 1, pc), ('vC', p, c, 2, pc)]
                    rv += [('vC', p, c, 0)]
                for v in range(3):
                    for g in range(2):
                        hh = 4 + 2 * p + g
                        B_.ld(BMc[0:64, p, v, g * 192:(g + 1) * 192],
                              bm_s[hh, OH_A + p * OH_C:OH_A + (p + 1) * OH_C].rearrange("(k n) -> k n", k=64),
                              ('BMc', p, v, g))
            B_.done()
            for p, r in enumerate(RS):
                for g in range(2):
                    S.op('dve', lambda h_, o_=BMc[0:64, p, 1, g * 192:g * 192 + 64]: h_.tensor_scalar(
                        out=o_, in0=o_, scalar1=cst[0:64, 222:223], scalar2=None, op0=ALU.add),
                        [('BMc', p, 1, g), 'cst'], [('BMc', p, 1, g)])
                    S.op('dve', lambda h_, o_=BMc[0:64, p, 2, g * 192 + 128:g * 192 + 192]: h_.tensor_scalar(
                        out=o_, in0=o_, scalar1=cst[0:64, 223:224], scalar2=None, op0=ALU.add),
                        [('BMc', p, 2, g), 'cst'], [('BMc', p, 2, g)])
            units = [(p, r, c, mb) for p, r in enumerate(RS) for c in range(r) for mb in range(32 // r)]

            def score(ui):
                p, r, c, mb = units[ui]
                nb = 32 // r
                hw = 64 * r
                var = 1 if mb == 0 else (2 if mb == nb - 1 else 0)
                bs = ui % 3
                l2 = ui % 2
                p3 = ui % 3
                q0 = mb * 64 * r + c
                for g in range(2):
                    for jj in range(3):
                        k0 = koff[p] + hw + (mb - 1 + jj) * 64 * r + c
                        mm(P[bs][0:64, g * 192 + jj * 64:g * 192 + (jj + 1) * 64],
                           kC[:, k0:k0 + 63 * r + 1:r], qC[:, 2 * p + g, q0:q0 + 63 * r + 1:r], True, True, rk,
                           [('P', bs)])
                dve_stt(lg[l2][0:64, :], P[bs][0:64, 0:384], SCALE, BMc[0:64, p, var, :], ALU.mult, ALU.add,
                        [('P', bs), ('BMc', p, var, 0), ('BMc', p, var, 1)], [('lg', l2)])
                act(pT[p3][0:64, :], lg[l2][0:64, :], AF.Exp, [('lg', l2), 'negcA'], [('pT', p3)],
                    bias=negcA[0:64, 0:1], scale=1.0)

            def finish(ui):
                p, r, c, mb = units[ui]
                nb = 32 // r
                base = vboff[p] + c * (nb + 2)
                bo = 3 + ui % 3
                p3 = ui % 3
                q0 = mb * 64 * r + c
                for g in range(2):
                    for jj in range(3):
                        mm(P[bo][:, g * 64:(g + 1) * 64], vC[0:64, base + mb + jj, :],
                           pT[p3][0:64, g * 192 + jj * 64:g * 192 + (jj + 1) * 64], jj == 0, jj == 2,
                           rv + [('pT', p3)], [('P', bo)])
                for g in range(2):
                    for jj in range(3):
                        mm(P[bo][:, 128 + g * 64:128 + (g + 1) * 64], ones_bf[0:64, :],
                           pT[p3][0:64, g * 192 + jj * 64:g * 192 + (jj + 1) * 64], jj == 0, jj == 2,
                           ['ones_bf', ('pT', p3)], [('P', bo)])
                accv = ACC[:, :, q0:q0 + 63 * r + 1:r]
                psv = P[bo][:, 0:256].rearrange("p (a n) -> p a n", a=4)
                if p == 0:
                    dve_copy(accv, psv, [('P', bo)], ['ACC'])
                else:
                    dve_tt(accv, accv, psv, ALU.add, [('P', bo), 'ACC'], ['ACC'])

            score(0)
            for ui in range(len(units)):
                if ui + 1 < len(units):
                    score(ui + 1)
                finish(ui)
            for g in range(2):
                dve_recip(rdc, ACC[:, 2 + g, :], ['ACC'], ['rdc'])
                dve_tt(yC[:, g, :], ACC[:, g, :], rdc, ALU.mult, ['ACC', 'rdc'], [('yC', g)], eng='pool')
                dma('sp', SL('yc'), y_s[10 + g], yC[:, g, :], [('yC', g)], [('y_s', 10 + g)])

        def merge_phase(l, tt):
            t0 = tt * T
            wgb = [carve(AR, i * 12288, 12288, BF16) for i in range(2)]
            wbrb = [carve(AR, 24576 + i * 3072, 3072, BF16) for i in range(2)]
            wob = [carve(AR, i * 4096, 4096, BF16) for i in range(4)]
            mg = carve(AR, 30720, 32768, BF16).rearrange("p (k n) -> p k n", k=KC)
            yT = carve(AR, 63488, 24576, BF16).rearrange("p (k n) -> p k n", k=12)
            gt = [sq[0][:, :], sq[1][:, :], sg[0][:, :]]
            tmp = [sg[1][:, :], rstd[:, 0:512], rstd[:, 512:1024]]
            dma('sp', SL('m1'), xn[:, :, 0:512], hn_s[:, :, t0:t0 + 512].rearrange("k p t -> p k t"), [],
                [('xn', k, 0) for k in range(KC)])
            dma('act', SL('m2'), yT[:, :, :], y_s[:, :, t0:t0 + T].rearrange("h p t -> p h t"), [], ['yT'])
            dma('sp', SL('m3'), xn[:, :, 512:T], hn_s[:, :, t0 + 512:t0 + T].rearrange("k p t -> p k t"), [],
                [('xn', k, 1) for k in range(KC)])
            def issue_w(m):
                s2 = m % 2
                for b in range(3):
                    dma('pool', SL('wg%d_%d' % (s2, b)), wgb[s2][:, b * 2048:(b + 1) * 2048],
                        wg_d[l * 48 + b * 16 + m], [], [('wg', s2, b)])
                dma('pool', SL('wbr%d' % s2), wbrb[s2], wbr_d[l * 16 + m], [], [('wbr', s2)])

            issue_w(0)
            for m in range(KC):
                s2 = m % 2
                if m + 1 < KC:
                    issue_w(m + 1)
                for half in range(2):
                    hs = slice(half * 512, (half + 1) * 512)
                    for b in range(3):
                        for k in range(KC):
                            mm(P[b][:, :], wgb[s2][:, b * 2048 + k * 128:b * 2048 + (k + 1) * 128], xn[:, k, hs],
                               k == 0, k == KC - 1, [('wg', s2, b), ('xn', k, half)], [('P', b)])
                        gc = l * CL + 48 + b * 16 + m
                        tsig = act(gt[b], P[b][:, :], AF.Sigmoid, [('P', b), 'cst'], [('gt', b)],
                                   bias=cst[:, gc:gc + 1], scale=1.0)
                        if m == 0 and half == 0 and b == 0:
                            S.dma('sp', SL('m0'), lambda h: h.dma_start(
                                out=xT[:, :, :], in_=xT_s[:, :, t0:t0 + T].rearrange("k p t -> p k t")), [],
                                [('xT', k_, h_) for k_ in range(KC) for h_ in range(2)], extra=[tsig])
                    for (b, k0, k1) in ((0, 0, 4), (1, 4, 10), (2, 10, 12)):
                        for kh in range(k0, k1):
                            mm(P[3 + b][:, :], wbrb[s2][:, kh * 128:(kh + 1) * 128], yT[:, kh, hs], kh == k0,
                               kh == k1 - 1, [('wbr', s2), 'yT'], [('P', 3 + b)])
                    for b in range(3):
                        dve_tt(tmp[b], gt[b], P[3 + b][:, :], ALU.mult, [('gt', b), ('P', 3 + b)], [('tmp', b)])
                    dve_tt(tmp[0], tmp[0], tmp[1], ALU.add, [('tmp', 0), ('tmp', 1)], [('tmp', 0)], eng='pool')
                    dve_tt(mg[:, m, hs], tmp[0], tmp[2], ALU.add, [('tmp', 0), ('tmp', 2)], [('mg', m, half)], eng='pool')
            S.barrier()
            for m in range(KC):
                s2 = m % 4
                dma('pool', SL('wo%d' % s2), wob[s2], wo_d[l * 16 + m], [], [('wo', s2)])
                for half in range(2):
                    hs = slice(half * 512, (half + 1) * 512)
                    b = 6 + (m * 2 + half) % 2
                    for k in range(KC):
                        mm(P[b][:, :], wob[s2][:, k * 128:(k + 1) * 128], mg[:, k, hs], k == 0, k == KC - 1,
                           [('wo', s2), ('mg', k, half)], [('P', b)])
                    dve_tt(xT[:, m, hs], P[b][:, :], xT[:, m, hs], ALU.add, [('P', b), ('xT', m, half)],
                           [('xT', m, half)])


        def final_out(tt):
            t0 = tt * T
            norm_stats()
            for k in range(KC):
                for half in range(2):
                    hs = slice(half * 512, (half + 1) * 512)
                    dve_stt(xT[:, k, hs], xT[:, k, hs], cst[:, 204 + k:205 + k], P[5 + half][:, :],
                            ALU.mult, ALU.mult, [('xT', k, half), ('P', 5 + half), 'cst'], [('xT', k, half)])
            ui = 0
            for tb in range(T // 128):
                s2 = tb % 2
                ost = carve(AR, s2 * 8192, 8192, F32)
                for k4 in range(4):
                    b = ui % 4
                    ui += 1
                    for kk in range(4):
                        k = k4 * 4 + kk
                        mm(P[b][:, kk * 128:(kk + 1) * 128], xT[:, k, tb * 128:(tb + 1) * 128], ident[:, :], True, True,
                           [('xT', k, tb // 4), 'ident'], [('P', b)])
                    if k4 % 2 == 0:
                        dve_copy(ost[:, k4 * 512:(k4 + 1) * 512], P[b][:, :], [('P', b)], [('ost', s2, k4)])
                    else:
                        act(ost[:, k4 * 512:(k4 + 1) * 512], P[b][:, :], AF.Copy, [('P', b)], [('ost', s2, k4)])
                r0 = t0 + tb * 128
                dma('sp', SL('ost%d' % s2), y_d[r0:r0 + 128, :], ost, [('ost', s2, k4) for k4 in range(4)],
                    [('y', tt, tb)])

        class _Stop(Exception):
            pass

        stage = [0]

        def step(fn, *a):
            stage[0] += 1
            if DEBUG_STAGE is not None and stage[0] > DEBUG_STAGE:
                raise _Stop()
            fn(*a)
            S.barrier()

        try:
            for tt in range(NTT):
                step(load_x_tile, tt)
                step(ffn, 0, 0)
                step(mix_in, 0, tt)
            for l in range(L):
                if l == 0:
                    exchange()
                    step(bm_compute)
                else:
                    step(exchange)
                step(attn_A, l)
                step(attn_B, l)
                step(attn_C, l)
                for tt in range(NTT):
                    step(merge_phase, l, tt)
                    if l + 1 < L:
                        ffn(l, 1)
                        step(ffn, l + 1, 0)
                    else:
                        step(ffn, l, 1)
                    if l + 1 < L:
                        step(mix_in, l + 1, tt)
                    else:
                        step(final_out, tt)
        except _Stop:
            pass
        S.finish(block)
    return nc


def _t5_bucket_np(rel):
    half = 16
    me = 8
    ret = np.where(rel > 0, half, 0)
    n = np.abs(rel)
    nf = np.maximum(n, 1).astype(np.float32)
    large = me + (np.log(nf / np.float32(me)) / np.float32(math.log(2048 / me)) * np.float32(half - me)).astype(np.int32)
    large = np.minimum(large, half - 1)
    return ret + np.where(n < me, n, large)


def _onehot_tables():
    oh = np.zeros((33, OH_N), np.float32)

    def fill(off, blk, dil):
        kl = np.arange(blk)[:, None]
        n = np.arange(3 * blk)[None, :]
        jj = n // blk
        ql = n % blk
        rel = (jj - 1) * blk + kl - ql
        valid = np.abs(rel) <= blk
        bk = _t5_bucket_np((rel * dil).astype(np.int32))
        cols = off + (kl * 3 * blk + n)
        for b in range(32):
            sel = valid & (bk == b)
            oh[b, cols[sel]] = 1.0
        oh[32, cols[~valid]] = 1.0

    fill(0, 128, 1)
    for p, dil in enumerate((1, 4, 16)):
        fill(OH_A + p * OH_C, 64, dil)
    return oh


def _rope_tables(base):
    pos = base + np.arange(NT)
    row = (pos // 64).astype(np.float32)
    col = (pos % 64).astype(np.float32)
    inv = (np.float32(10000.0) ** (-(np.arange(0, 64, 2, dtype=np.float32) / np.float32(64)))).astype(np.float32)
    ar = (row[:, None] * inv[None, :]).astype(np.float32)
    ac = (col[:, None] * inv[None, :]).astype(np.float32)
    C = np.zeros((128, NT), np.float32)
    Sg = np.zeros((128, NT), np.float32)
    for a0, ang in ((0, ar), (64, ac)):
        c = np.cos(ang).astype(np.float32).T
        s = np.sin(ang).astype(np.float32).T
        C[a0:a0 + 32] = c
        C[a0 + 32:a0 + 64] = c
        Sg[a0:a0 + 32] = -s
        Sg[a0 + 32:a0 + 64] = s
    return C, Sg


_NC_CACHE = {}


def kernel(x_prompt, x_sample, ffn1_norm, ffn1_w13, ffn1_w2, mix_norm, w_in, q_gain_b, k_gain_b,
           sink_a, w_gate, b_gate, w_br_a, w_br_b, w_br_c, w_o, ffn2_norm, ffn2_w13, ffn2_w2,
           rel_bias, final_norm):
    f32 = np.float32
    A = lambda a: np.ascontiguousarray(np.asarray(a, dtype=f32))
    shared = {}

    def w13_layout(w):
        a = np.asarray(w, f32).reshape(L, 16, 128, 2, FC, 128).transpose(0, 4, 2, 1, 3, 5)
        return np.ascontiguousarray(a).reshape(L * FC, 128, 4096)

    shared["w13_1"] = w13_layout(ffn1_w13)
    shared["w13_2"] = w13_layout(ffn2_w13)
    shared["w2_1"] = A(ffn1_w2).reshape(L * FC, 128, D)
    shared["w2_2"] = A(ffn2_w2).reshape(L * FC, 128, D)
    win = np.asarray(w_in, f32)
    qk_starts = ([0, 128, 256, 384, 512, 640] + [1024 + 128 * i for i in range(6)] + [1792, 1920] +
                 [2304 + 128 * i for i in range(6)] + [3072, 3200, 3328])
    qk_cols = np.concatenate([np.arange(s, s + 128) for s in qk_starts])
    a = win[:, :, qk_cols].reshape(L, 16, 128, 23, 128).transpose(0, 3, 2, 1, 4)
    shared["wqk"] = np.ascontiguousarray(a).reshape(L * 23, 128, 2048)
    v_cols = np.concatenate([np.arange(768, 1024), np.arange(2048, 2304), np.arange(3456, 3840)])
    a = win[:, :, v_cols].reshape(L, 16, 128, 896).transpose(0, 2, 1, 3)
    shared["wv"] = np.ascontiguousarray(a).reshape(L, 128, 16 * 896)
    a = np.asarray(w_gate, f32).reshape(L, 16, 128, 48, 128).transpose(0, 3, 2, 1, 4)
    shared["wg"] = np.ascontiguousarray(a).reshape(L * 48, 128, 2048)
    wbr = np.concatenate([np.asarray(w_br_a, f32), np.asarray(w_br_b, f32), np.asarray(w_br_c, f32)], axis=1)
    a = wbr.reshape(L, 12, 128, 16, 128).transpose(0, 3, 2, 1, 4)
    shared["wbr"] = np.ascontiguousarray(a).reshape(L * 16, 128, 1536)
    a = np.asarray(w_o, f32).reshape(L, 16, 128, 16, 128).transpose(0, 3, 2, 1, 4)
    shared["wo"] = np.ascontiguousarray(a).reshape(L * 16, 128, 2048)
    shared["oh"] = _onehot_tables()
    shared["rbx"] = np.concatenate([np.asarray(rel_bias, f32), np.full((1, 10), NEG, f32)], axis=0)
    pm = np.zeros((128, 128), f32)
    for m in range(128):
        pm[(m + 32) if (m % 64) < 32 else (m - 32), m] = 1.0
    shared["perm"] = pm
    shared["ident"] = np.eye(128, dtype=f32)

    cst0 = np.zeros((128, NCST), f32)
    for l in range(L):
        b = l * CL
        cst0[:, b:b + 16] = np.asarray(ffn1_norm, f32)[l].reshape(16, 128).T
        cst0[:, b + 16:b + 32] = np.asarray(mix_norm, f32)[l].reshape(16, 128).T
        cst0[:, b + 32:b + 48] = np.asarray(ffn2_norm, f32)[l].reshape(16, 128).T
        cst0[:, b + 48:b + 96] = np.asarray(b_gate, f32)[l].reshape(48, 128).T
        cst0[:, b + 96] = np.asarray(q_gain_b, f32)[l]
        cst0[:, b + 97] = np.asarray(k_gain_b, f32)[l]
        cst0[:, b + 98:b + 102] = np.asarray(sink_a, f32)[l][None, :]
    cst0[:, 204:220] = np.asarray(final_norm, f32).reshape(16, 128).T

    xp = np.asarray(x_prompt, f32)
    xs = np.asarray(x_sample, f32)
    in_maps = []
    ropes = {0: _rope_tables(0), NT: _rope_tables(NT)}
    for c in range(8):
        m = dict(shared)
        cst = cst0.copy()
        if c < 4:
            m["x"] = np.ascontiguousarray(xp[c])
            base = 0
            rank = c % 2
            flags = [0.0 if rank == 0 else NEG, 0.0 if rank == 1 else NEG, NEG, NEG]
        else:
            sidx = (c - 4) // 2
            rank = c % 2
            m["x"] = np.ascontiguousarray(xs[sidx, rank * NT:(rank + 1) * NT])
            base = rank * NT
            flags = [0.0, 0.0, 0.0 if rank == 1 else NEG, 0.0 if rank == 0 else NEG]
        cst[:, 220:224] = np.asarray(flags, f32)[None, :]
        m["cst"] = cst
        m["rope_c"], m["rope_s"] = ropes[base]
        in_maps.append(m)

    if "nc" not in _NC_CACHE:
        _NC_CACHE["nc"] = build()
    nc = _NC_CACHE["nc"]
    res = run_bass_kernel_spmd(nc, in_maps, core_ids=list(range(8)))
    ys = [np.asarray(res.results[c]["y"], dtype=f32) for c in range(8)]
    y_prompt = np.stack(ys[0:4], axis=0)
    y_sample = np.stack([np.concatenate(ys[4:6], axis=0), np.concatenate(ys[6:8], axis=0)], axis=0)
    return (y_prompt, y_sample)
```

```python
import math
from contextlib import ExitStack
import numpy as np
import concourse.bass as bass
import concourse.mybir as mybir
from concourse.bass_utils import run_bass_kernel_spmd

F32 = mybir.dt.float32
BF16 = mybir.dt.bfloat16
ALU = mybir.AluOpType
AF = mybir.ActivationFunctionType

ENGS = ['pe', 'act', 'dve', 'pool', 'sp']

L = 2
D = 2048
DFF = 5632
NT = 2048
T = 1024
NTT = NT // T
KC = 16
FC = 44
GC = 4
NG = FC // GC
CL = 102
NCST = 224
NEG = -30000.0
EPS = 1e-6
SCALE = 1.0 / math.sqrt(128.0)
OH_A = 128 * 384
OH_C = 64 * 192
OH_N = OH_A + 3 * OH_C
C_B = 12.0
DEBUG_STAGE = None
DEBUG_SKIP = set()
ROPE_ENG = 'pool'
DEBUG_RG = None


class _Op:
    __slots__ = ('fn', 'deps', 'need', 'val', 'dma', 'eng')


class DmaSlot:
    def __init__(self, sem, inc=16):
        self.sem = sem
        self.count = 0
        self.inc = inc


class Sched:
    def __init__(self, nc, stack):
        self.nc = nc
        self.stack = stack
        self.ops = {e: [] for e in ENGS}
        self.res = {}
        self.esem = {e: stack.enter_context(nc.semaphore('es_' + e)) for e in ENGS if e != 'sp'}
        self.nslots = 0

    def slot(self, inc=16):
        self.nslots += 1
        return DmaSlot(self.stack.enter_context(self.nc.semaphore('ds%d' % self.nslots)), inc)

    def _deps(self, reads, writes):
        deps = []
        for r in reads:
            st = self.res.get(r)
            if st and st[0] is not None:
                deps.append(st[0])
        for w in writes:
            st = self.res.get(w)
            if st:
                if st[0] is not None:
                    deps.append(st[0])
                deps.extend(st[1].values())
                deps.extend(st[2])
        return deps

    def _update(self, tok, eng, reads, writes, is_dma):
        for r in reads:
            st = self.res.setdefault(r, [None, {}, []])
            if is_dma:
                st[2].append(tok)
            else:
                st[1][eng] = tok
        for w in writes:
            self.res[w] = [tok, {}, []]

    def op(self, eng, fn, reads=(), writes=(), extra=()):
        o = _Op()
        o.fn = fn
        o.eng = eng
        o.deps = self._deps(reads, writes) + list(extra)
        o.need = False
        o.val = None
        o.dma = None
        self.ops[eng].append(o)
        tok = ('e', o)
        self._update(tok, eng, reads, writes, False)
        return tok

    def dma(self, queue, slot, fn, reads=(), writes=(), extra=()):
        o = _Op()
        o.fn = fn
        o.eng = queue
        o.deps = self._deps(reads, writes) + list(extra)
        o.need = False
        slot.count += slot.inc
        o.dma = (slot.sem, slot.count, slot.inc)
        o.val = None
        self.ops[queue].append(o)
        tok = ('d', slot.sem, slot.count)
        self._update(tok, queue, reads, writes, True)
        return tok

    def join(self, slot, names):
        tok = ('d', slot.sem, slot.count)
        for n in names:
            self.res[n] = [tok, {}, []]

    def barrier(self):
        toks = []
        for e in ENGS:
            lst = self.ops[e]
            for o in reversed(lst):
                if o.dma is None and o.fn is not None:
                    toks.append(('e', o))
                    break
            last = {}
            for o in lst:
                if o.dma is not None:
                    last[id(o.dma[0])] = ('d', o.dma[0], o.dma[1])
            toks.extend(last.values())
        for e in ENGS:
            self.op(e, None, extra=toks)
        self.res = {}

    def finish(self, block):
        for e in ENGS:
            for o in self.ops[e]:
                for d in o.deps:
                    if d[0] == 'e' and not (d[1].eng == 'pe' and e == 'pe'):
                        d[1].need = True
        for e in ENGS:
            c = 0
            for o in self.ops[e]:
                if o.dma is None and o.need:
                    c += 1
                    o.val = c
        sched = self

        def replay(ename, h):
            seen = {}
            for o in sched.ops[ename]:
                waits = {}
                for d in o.deps:
                    if d[0] == 'e':
                        od = d[1]
                        if od.eng == 'pe' and ename == 'pe':
                            continue
                        sem, val = sched.esem[od.eng], od.val
                    else:
                        sem, val = d[1], d[2]
                    k = id(sem)
                    if seen.get(k, 0) >= val:
                        continue
                    if k not in waits or waits[k][1] < val:
                        waits[k] = (sem, val)
                for k, (sem, val) in waits.items():
                    h.wait_ge(sem, val)
                    seen[k] = val
                if o.fn is None:
                    continue
                ins = o.fn(h)
                if o.dma is not None:
                    ins.then_inc(o.dma[0], o.dma[2])
                elif o.need:
                    ins.then_inc(sched.esem[ename], 1)

        @block.tensor
        def _(h):
            replay('pe', h)

        @block.scalar
        def _(h):
            replay('act', h)

        @block.vector
        def _(h):
            replay('dve', h)

        @block.gpsimd
        def _(h):
            replay('pool', h)

        @block.sync
        def _(h):
            replay('sp', h)


def build():
    nc = bass.Bass("TRN2", target_bir_lowering=False)
    dbg = DEBUG_STAGE is not None

    def din(name, shape, dt=F32):
        return nc.dram_tensor(name, shape, dt, kind="ExternalInput")

    x_d = din("x", [NT, D])
    w13_d = [din("w13_1", [L * FC, 128, 4096]), din("w13_2", [L * FC, 128, 4096])]
    w2_d = [din("w2_1", [L * FC, 128, D]), din("w2_2", [L * FC, 128, D])]
    wqk_d = din("wqk", [L * 23, 128, 2048])
    wv_d = din("wv", [L, 128, 16 * 896])
    wg_d = din("wg", [L * 48, 128, 2048])
    wbr_d = din("wbr", [L * 16, 128, 1536])
    wo_d = din("wo", [L * 16, 128, 2048])
    cst_d = din("cst", [128, NCST])
    ropec_d = din("rope_c", [128, NT])
    ropes_d = din("rope_s", [128, NT])
    oh_d = din("oh", [33, OH_N])
    rbx_d = din("rbx", [33, 10])
    perm_d = din("perm", [128, 128])
    ident_d = din("ident", [128, 128])
    y_d = nc.dram_tensor("y", [NT, D], F32, kind="ExternalOutput")

    dk = dict(kind="ExternalOutput") if dbg else {}
    xT_s = nc.dram_tensor("xT_s", [KC, 128, NT], F32, **dk)
    hn_s = nc.dram_tensor("hn_s", [KC, 128, NT], BF16, **dk)
    q_s = nc.dram_tensor("q_s", [16, 128, NT], BF16, **dk)
    kT_loc = nc.dram_tensor("kT_loc", [7 * 128, NT], BF16)
    kT_all = nc.dram_tensor("kT_all", [2 * 7 * 128, NT], BF16)
    v_loc = nc.dram_tensor("v_loc", [NT, 896], BF16)
    v_all = nc.dram_tensor("v_all", [2 * NT, 896], BF16)
    y_s = nc.dram_tensor("y_s", [12, 128, NT], BF16, **dk)
    bm_s = nc.dram_tensor("bm_s", [10, OH_N], F32, **dk)

    with ExitStack() as st:
        S = Sched(nc, st)

        def sb(name, shape, dt):
            return st.enter_context(nc.sbuf_tensor("sb_" + name, shape, dt))

        xT = sb("xT", [128, KC, T], F32)
        xn = sb("xn", [128, KC, T], BF16)
        AR = sb("AR", [128, 22528], F32)
        cst = sb("cst", [128, NCST], F32)
        onesD = sb("onesD", [128, 128], F32)
        onesH = sb("onesH", [128, 128], F32)
        ident = sb("ident", [128, 128], F32)
        perm = sb("perm", [128, 128], F32)
        ones_bf = sb("ones_bf", [128, 128], BF16)
        onesD_bf = sb("onesD_bf", [128, 128], BF16)
        epsb = sb("epsb", [128, 1], F32)
        negcA = sb("negcA", [128, 1], F32)
        negcB = sb("negcB", [128, 2], F32)
        sinkexp = sb("sinkexp", [128, 4], F32)
        rstd = sb("rstd", [128, T], F32)
        sq = [sb("sq%d" % i, [128, 512], F32) for i in range(2)]
        sg = [sb("sg%d" % i, [128, 512], F32) for i in range(2)]
        P = [st.enter_context(nc.psum_tensor("P%d" % i, [128, 512], F32)) for i in range(8)]
        block = st.enter_context(nc.Block())

        xT_flat = xT[:, :, :].rearrange("p k n -> p (k n)")
        xn_flat = xn[:, :, :].rearrange("p k n -> p (k n)")

        def carve(base, off_bytes, nbytes, dt):
            assert off_bytes % 4 == 0 and nbytes % 4 == 0
            v = base[:, off_bytes // 4:(off_bytes + nbytes) // 4]
            if dt == BF16:
                v = v.bitcast(BF16)
            return v

        def carve_bf(base_bf, off_el, n_el):
            return base_bf[:, off_el:off_el + n_el]

        def mm(out, lhsT, rhs, start, stop, reads, writes):
            S.op('pe', lambda h: h.matmul(out, lhsT=lhsT, rhs=rhs, start=start, stop=stop), reads, writes)

        def act(out, in_, func, reads, writes, bias=None, scale=None):
            kw = {}
            if bias is not None:
                kw['bias'] = bias
            if scale is not None:
                kw['scale'] = scale
            return S.op('act', lambda h: h.activation(out=out, in_=in_, func=func, **kw), reads, writes)

        def dve_tt(out, in0, in1, op, reads, writes, eng='dve'):
            S.op(eng, lambda h: h.tensor_tensor(out=out, in0=in0, in1=in1, op=op), reads, writes)

        def dve_stt(out, in0, scalar, in1, op0, op1, reads, writes, eng='dve'):
            S.op(eng, lambda h: h.scalar_tensor_tensor(out=out, in0=in0, scalar=scalar, in1=in1, op0=op0, op1=op1),
                 reads, writes)

        def dve_copy(out, in_, reads, writes, eng='dve'):
            S.op(eng, lambda h: h.tensor_copy(out=out, in_=in_), reads, writes)

        def dve_recip(out, in_, reads, writes):
            S.op('dve', lambda h: h.reciprocal(out=out, in_=in_), reads, writes)

        def dma(queue, slot, out, in_, reads, writes):
            return S.dma(queue, slot, lambda h: h.dma_start(out=out, in_=in_), reads, writes)

        def memset(ap, val, writes, eng='dve'):
            S.op(eng, lambda h: h.memset(ap, val), (), writes)

        slots = {}

        def SL(name):
            if name not in slots:
                slots[name] = S.slot()
            return slots[name]

        dma('sp', SL('c0'), cst[:, :], cst_d[:, :], [], ['cst'])
        dma('sp', SL('c1'), ident[:, :], ident_d[:, :], [], ['ident'])
        dma('sp', SL('c2'), perm[:, :], perm_d[:, :], [], ['perm'])
        memset(onesD[:, :], 1.0 / D, ['onesD'])
        memset(onesH[:, :], 1.0 / 128, ['onesH'])
        memset(ones_bf[:, :], 1.0, ['ones_bf'])
        memset(onesD_bf[:, :], 1.0 / D, ['onesD_bf'])
        memset(epsb[:, :], EPS, ['epsb'])
        memset(negcA[:, :], 0.0, ['negcA'])
        S.op('dve', lambda h: h.tensor_scalar(out=negcB[:, 0:2], in0=cst[:, 220:222], scalar1=-C_B, scalar2=None,
                                              op0=ALU.add), ['cst'], ['negcB'])

        def bm_compute():
            rbx = carve(AR, 0, 64, F32)[0:33, 0:10]
            dma('sp', SL('c3'), rbx, rbx_d[:, :], [], ['rbx'])
            PIECE = 4096
            npieces = OH_N // PIECE
            for pi in range(npieces):
                s2 = pi % 2
                ohb = carve(AR, 1024 + s2 * PIECE * 4, PIECE * 4, F32)
                stg = carve(AR, 1024 + 2 * PIECE * 4 + s2 * PIECE * 4, PIECE * 4, F32)
                dma('sp', SL('oh%d' % s2), ohb[0:33, :], oh_d[:, pi * PIECE:(pi + 1) * PIECE], [], [('oh', s2)])
                for q in range(PIECE // 512):
                    b = q % 4
                    mm(P[b][0:10, :], rbx, ohb[0:33, q * 512:(q + 1) * 512], True, True,
                       ['rbx', ('oh', s2)], [('P', b)])
                    if q % 2 == 0:
                        dve_copy(stg[0:10, q * 512:(q + 1) * 512], P[b][0:10, :], [('P', b)], [('stg', s2, q)])
                    else:
                        act(stg[0:10, q * 512:(q + 1) * 512], P[b][0:10, :], AF.Copy, [('P', b)], [('stg', s2, q)])
                dma('sp', SL('bmst%d' % s2), bm_s[:, pi * PIECE:(pi + 1) * PIECE], stg[0:10, :],
                    [('stg', s2, q) for q in range(PIECE // 512)], [('bm_s', pi)])

        S.barrier()

        def load_x_tile(tt):
            for tb in range(T // 128):
                s2 = tb % 2
                xin = carve(AR, s2 * 8192, 8192, F32)
                r0 = tt * T + tb * 128
                dma('sp', SL('xin%d' % s2), xin, x_d[r0:r0 + 128, :], [], [('xin', s2)])
                for k4 in range(4):
                    b = (tb * 4 + k4) % 4
                    for kk in range(4):
                        k = k4 * 4 + kk
                        mm(P[b][:, kk * 128:(kk + 1) * 128], xin[:, k * 128:(k + 1) * 128], ident[:, :], True, True,
                           [('xin', s2), 'ident'], [('P', b)])
                    outv = xT[:, k4 * 4:(k4 + 1) * 4, tb * 128:(tb + 1) * 128]
                    inv = P[b][:, :].rearrange("p (a b) -> p a b", a=4)
                    wr = [('xT', k4 * 4 + kk, tb // 4) for kk in range(4)]
                    if k4 % 2 == 0:
                        dve_copy(outv, inv, [('P', b)], wr)
                    else:
                        act(outv, inv, AF.Copy, [('P', b)], wr)

        def norm_stats():
            for half in range(2):
                for k in range(KC):
                    s2 = k % 2
                    sqb = sq[s2][:, 0:256].bitcast(BF16)
                    act(sqb, xT[:, k, half * 512:(half + 1) * 512], AF.Square,
                        [('xT', k, half)], [('sq', s2)])
                    mm(P[7][:, :], onesD_bf[:, :], sqb, k == 0, k == KC - 1,
                       [('sq', s2), 'onesD_bf'], [('P', 7)])
                act(rstd[:, half * 512:(half + 1) * 512], P[7][:, :], AF.Sqrt, [('P', 7), 'epsb'], [('rstd', half)],
                    bias=epsb[:, 0:1], scale=1.0)
                dve_recip(P[5 + half][:, :], rstd[:, half * 512:(half + 1) * 512],
                          [('rstd', half)], [('P', 5 + half)])

        def norm_to_xn(gcol):
            norm_stats()
            for k in range(KC):
                for half in range(2):
                    hs = slice(half * 512, (half + 1) * 512)
                    dve_stt(xn[:, k, hs], xT[:, k, hs], cst[:, gcol + k:gcol + k + 1], P[5 + half][:, :],
                            ALU.mult, ALU.mult, [('xT', k, half), ('P', 5 + half), 'cst'], [('xn', k, half)])

        def ffn(l, which):
            w13b = [carve(AR, i * 8192, 8192, BF16) for i in range(4)]
            w2b = [carve(AR, 32768 + i * 16384, 16384, BF16) for i in range(2)]
            hTb = [carve(AR, 65536 + i * 8192, 8192, BF16) for i in range(2)]
            norm_to_xn(l * CL + (0 if which == 0 else 32))

            def second(g):
                gs = g % 2
                for m in range(KC):
                    for half in range(2):
                        b = 4 + (m * 2 + half) % 3
                        for ci in range(GC):
                            mm(P[b][:, :], w2b[gs][:, ci * 2048 + m * 128:ci * 2048 + (m + 1) * 128],
                               hTb[gs][:, ci * 1024 + half * 512:ci * 1024 + (half + 1) * 512],
                               ci == 0, ci == GC - 1, [('w2', gs), ('hT', gs, ci, half)], [('P', b)])
                        xs = xT[:, m, half * 512:(half + 1) * 512]
                        dve_stt(xs, P[b][:, :], 0.5, xs, ALU.mult, ALU.add, [('P', b), ('xT', m, half)],
                                [('xT', m, half)])

            pending = None
            for g in range(NG):
                gs = g % 2
                for ci in range(GC):
                    c = g * GC + ci
                    s4 = c % 4
                    dma('pool', SL('w13_%d' % s4), w13b[s4].rearrange("p (a b) -> p a b", a=2),
                        w13_d[which][l * FC + c].rearrange("p (a b) -> p a b", a=2), [], [('w13', s4)])
                    for half in range(2):
                        u = (c * 2 + half) % 2
                        for gu in range(2):
                            b = 2 * u + gu
                            for k in range(KC):
                                mm(P[b][:, :], w13b[s4][:, k * 256 + gu * 128:k * 256 + (gu + 1) * 128],
                                   xn[:, k, half * 512:(half + 1) * 512], k == 0, k == KC - 1,
                                   [('w13', s4), ('xn', k, half)], [('P', b)])
                        act(sg[u][:, :], P[2 * u][:, :], AF.Silu, [('P', 2 * u)], [('sg', u)])
                        dve_tt(hTb[gs][:, ci * 1024 + half * 512:ci * 1024 + (half + 1) * 512], sg[u][:, :],
                               P[2 * u + 1][:, :], ALU.mult, [('sg', u), ('P', 2 * u + 1)], [('hT', gs, ci, half)])
                dma('pool', SL('w2_%d' % gs), w2b[gs].rearrange("p (c n) -> p c n", c=GC),
                    w2_d[which][l * FC + g * GC:l * FC + (g + 1) * GC].rearrange("c p n -> p c n"), [],
                    [('w2', gs)])
                if pending is not None:
                    second(pending)
                pending = g
            second(pending)

        QK_DEST = ([('q', i) for i in range(4)] + [('k', 0), ('k', 1)] + [('q', 4 + i) for i in range(6)] +
                   [('k', 2), ('k', 3)] + [('q', 10 + i) for i in range(6)] + [('k', 4), ('k', 5), ('k', 6)])

        def mix_in(l, tt):
            t0 = tt * T
            wqb = [carve(AR, 69632 + i * 4096, 4096, BF16) for i in range(3)]
            wvb = carve(AR, 8192, 28672, BF16)
            zq = [carve(AR, 36864 + i * 2048, 2048, F32) for i in range(2)]
            sq2 = [carve(AR, 40960 + i * 2048, 2048, F32) for i in range(2)]
            rs2 = [carve(AR, 45056 + i * 2048, 2048, F32) for i in range(2)]
            qn = [carve(AR, 49152 + i * 2048, 2048, F32) for i in range(2)]
            r1 = [carve(AR, 53248 + i * 2048, 2048, F32) for i in range(2)]
            r2 = [carve(AR, 57344 + i * 2048, 2048, F32) for i in range(2)]
            zst = [carve(AR, 61440 + i * 1024, 1024, BF16) for i in range(4)]
            vst = [carve(AR, 65536 + i * 1792, 1792, BF16) for i in range(2)]
            ropec = carve(AR, 0, 4096, F32)
            ropes = carve(AR, 4096, 4096, F32)
            if 'm' in DEBUG_SKIP:
                memset(ropec, 0.0, ['ropec'])
                memset(ropes, 0.0, ['ropes'])

            norm_to_xn(l * CL + 16)
            if 'a' not in DEBUG_SKIP:
                dma('sp', SL('xTst'), xT_s[:, :, t0:t0 + T].rearrange("k p t -> p k t"), xT[:, :, :],
                    [('xT', k, h_) for k in range(KC) for h_ in range(2)], [('xT_s', tt)])
            if 'b' not in DEBUG_SKIP:
                dma('sp', SL('hnst'), hn_s[:, :, t0:t0 + T].rearrange("k p t -> p k t"), xn[:, :, :],
                    [('xn', k, h_) for k in range(KC) for h_ in range(2)], [('hn_s', tt)])
            dma('sp', SL('ropec'), ropec, ropec_d[:, t0:t0 + T], [], ['ropec'])
            dma('sp', SL('ropes'), ropes, ropes_d[:, t0:t0 + T], [], ['ropes'])
            dma('pool', SL('wv'), wvb.rearrange("p (k n) -> p k n", k=KC),
                wv_d[l].rearrange("p (k n) -> p k n", k=KC), [], ['wv'])
            ui = 0

            def issue_qk(j):
                dma('pool', SL('wqk%d' % (j % 3)), wqb[j % 3], wqk_d[l * 23 + j], [], [('wqk', j % 3)])

            issue_qk(0)
            issue_qk(1)
            pend = []
            kst = []
            for j in range(23):
                s2 = j % 3
                if j + 2 < 23:
                    issue_qk(j + 2)
                kind, idx = QK_DEST[j]
                isB = (6 <= j <= 13) and 'h' not in DEBUG_SKIP
                for half in range(2):
                    b = ui % 3
                    z4 = ui % 4
                    r_ = ui % 2
                    ui += 1
                    for k in range(KC):
                        mm(P[b][:, :], wqb[s2][:, k * 128:(k + 1) * 128], xn[:, k, half * 512:(half + 1) * 512],
                           k == 0, k == KC - 1, [('wqk', s2), ('xn', k, half)], [('P', b)])
                    if not isB:
                        act(zst[z4], P[b][:, :], AF.Copy, [('P', b)], [('zst', z4)])
                    else:
                        gcol = l * CL + 96 + (0 if j <= 11 else 1)
                        act(zq[r_], P[b][:, :], AF.Copy, [('P', b), 'cst'], [('zq', r_)], scale=cst[:, gcol:gcol + 1])
                        act(sq2[r_], P[b][:, :], AF.Square, [('P', b)], [('sq2', r_)])
                        mm(P[3][:, :], onesH[:, :], sq2[r_], True, True, [('sq2', r_), 'onesH'], [('P', 3)])
                        act(rs2[r_], P[3][:, :], AF.Sqrt, [('P', 3), 'epsb'], [('rs2', r_)], bias=epsb[:, 0:1],
                            scale=1.0)
                        dve_recip(rs2[r_], rs2[r_], [('rs2', r_)], [('rs2', r_)])
                        dve_tt(qn[r_], zq[r_], rs2[r_], ALU.mult, [('zq', r_), ('rs2', r_)], [('qn', r_)], eng=ROPE_ENG)

                        def tail(r_=r_, z4=z4, half=half, kind=kind, idx=idx):
                            mm(P[4][:, :], perm[:, :], qn[r_], True, True, [('qn', r_), 'perm'], [('P', 4)])
                            dve_tt(r1[r_], qn[r_], ropec[:, half * 512:(half + 1) * 512], ALU.mult,
                                   [('qn', r_), 'ropec'], [('r1', r_)], eng=ROPE_ENG)
                            dve_tt(r2[r_], P[4][:, :], ropes[:, half * 512:(half + 1) * 512], ALU.mult,
                                   [('P', 4), 'ropes'], [('r2', r_)])
                            dve_tt(zst[z4], r1[r_], r2[r_], ALU.add, [('r1', r_), ('r2', r_)], [('zst', z4)],
                                   eng=ROPE_ENG)
                            c0_ = t0 + half * 512
                            if kind == 'q':
                                dst_ = q_s[idx, :, c0_:c0_ + 512]
                            else:
                                dst_ = kT_loc[idx * 128:(idx + 1) * 128, c0_:c0_ + 512]
                            tk_ = dma('sp', SL('zst%d' % z4), dst_, zst[z4], [('zst', z4)], [(kind, idx, tt, half)])
                            if kind == 'k':
                                kst.append(tk_)

                        pend.append(tail)
                        if len(pend) > 1:
                            pend.pop(0)()
                        continue
                    while pend:
                        pend.pop(0)()
                    c0 = t0 + half * 512
                    if kind == 'q':
                        dst = q_s[idx, :, c0:c0 + 512]
                    else:
                        dst = kT_loc[idx * 128:(idx + 1) * 128, c0:c0 + 512]
                    tk_ = dma('sp', SL('zst%d' % z4), dst, zst[z4], [('zst', z4)], [(kind, idx, tt, half)])
                    if kind == 'k':
                        kst.append(tk_)
            while pend:
                pend.pop(0)()
            if tt == NTT - 1:
                exchange_k(kst)
            for tb in range(T // 128):
                if 'e' in DEBUG_SKIP:
                    break
                s2 = tb % 2
                for (n0, n1, b) in ((0, 512, 2 * s2), (512, 896, 2 * s2 + 1)):
                    for k in range(KC):
                        mm(P[b][:, 0:n1 - n0], xn[:, k, tb * 128:(tb + 1) * 128],
                           wvb[:, k * 896 + n0:k * 896 + n1], k == 0, k == KC - 1, ['wv', ('xn', k, tb // 4)], [('P', b)])
                act(vst[s2][:, 0:512], P[2 * s2][:, :], AF.Copy, [('P', 2 * s2)], [('vst', s2, 0)])
                dve_copy(vst[s2][:, 512:896], P[2 * s2 + 1][:, 0:384], [('P', 2 * s2 + 1)], [('vst', s2, 1)])
                r0 = t0 + tb * 128
                dma('sp', SL('vst%d' % s2), v_loc[r0:r0 + 128, :], vst[s2], [('vst', s2, 0), ('vst', s2, 1)],
                    [('v_loc', tt, tb)])

        cc_slot = S.slot(inc=1)

        cc_state = {'t1': None}

        def exchange_k(extra):
            RG = DEBUG_RG or [[0, 1], [2, 3], [4, 5], [6, 7]]
            t1 = None
            for ch in range(7):
                t1 = S.dma('pool', cc_slot, lambda h, ch=ch: h.collective_compute(
                    "AllGather", ALU.bypass, replica_groups=RG, ins=[kT_loc[ch * 128:(ch + 1) * 128, :].opt()],
                    outs=[kT_all[ch * 256:(ch + 1) * 256, :].opt()]), extra=([t1] if t1 else list(extra)))
            cc_state['t1'] = t1

        def exchange():
            RG = DEBUG_RG or [[0, 1], [2, 3], [4, 5], [6, 7]]
            t1 = cc_state['t1']
            for vc in range(8):
                t1 = S.dma('pool', cc_slot, lambda h, vc=vc: h.collective_compute(
                    "AllGather", ALU.bypass, replica_groups=RG, ins=[v_loc[vc * 256:(vc + 1) * 256, :].opt()],
                    outs=[v_all[vc * 512:(vc + 1) * 512, :].opt()]), extra=([t1] if t1 else []))
            cc_state['t1'] = None

        KALL = kT_all.ap().rearrange("(c s p) n -> s p c n", c=7, s=2, p=128)
        VALL = v_all.ap().rearrange("(c s i) d -> s c i d", c=8, s=2, i=256)

        xn_bf = xn_flat
        AR_bf = AR[:, :].bitcast(BF16)
        xT_bf = xT_flat.bitcast(BF16)

        class Batch:
            def __init__(self, name):
                self.slot = SL(name)
                self.names = []

            def ld(self, out, in_, name):
                q = 'sp' if len(self.names) % 2 == 0 else 'act'
                dma(q, self.slot, out, in_, [], [name])
                self.names.append(name)

            def done(self):
                S.join(self.slot, self.names)

        def attn_A(l):
            qA = xn_bf[:, 0:4 * NT].rearrange("p (h n) -> p h n", h=4)
            kA = AR_bf[:, 0:2 * 2304].rearrange("p (h n) -> p h n", h=2)
            vA = AR_bf[:, 4608:4608 + 18 * 256].rearrange("p (b c) -> p b c", b=18)
            yA = AR_bf[:, 9216:9216 + 4 * NT].rearrange("p (h n) -> p h n", h=4)
            BM = xT_flat[:, 0:4 * 3 * 384].rearrange("p (h v n) -> p h v n", h=4, v=3)
            lg = [xT_flat[:, 4608 + i * 384:4608 + (i + 1) * 384] for i in range(2)]
            pT = [xT_bf[:, 2 * 5376 + i * 384:2 * 5376 + (i + 1) * 384] for i in range(3)]
            dd = [xT_flat[:, 6144 + i * 128:6144 + (i + 1) * 128] for i in range(2)]
            B_ = Batch('ldA')
            B_.ld(qA, q_s[0:4].rearrange("h p n -> p h n"), 'qA')
            B_.ld(kA[:, :, 128:128 + NT], kT_loc[0:256, :].rearrange("(h p) n -> p h n", p=128), 'kA0')
            B_.ld(kA[:, :, 0:128], KALL[0][:, 0:2, NT - 128:NT], 'kA1')
            B_.ld(kA[:, :, 128 + NT:256 + NT], KALL[1][:, 0:2, 0:128], 'kA2')
            B_.ld(vA[:, 1:17, :], v_loc[:, 0:256].rearrange("(b p) c -> p b c", p=128), 'vA0')
            B_.ld(vA[:, 0, :], VALL[0][7, 128:256, 0:256], 'vA1')
            B_.ld(vA[:, 17, :], VALL[1][0, 0:128, 0:256], 'vA2')
            for h in range(4):
                for v in range(3):
                    B_.ld(BM[:, h, v, :], bm_s[h, 0:OH_A].rearrange("(k n) -> k n", k=128), ('BM', h, v))
            B_.done()
            for h in range(4):
                S.op('dve', lambda h_, h=h: h_.tensor_scalar(out=BM[:, h, 1, 0:128], in0=BM[:, h, 1, 0:128],
                                                            scalar1=cst[:, 222:223], scalar2=None, op0=ALU.add),
                     [('BM', h, 1), 'cst'], [('BM', h, 1)])
                S.op('dve', lambda h_, h=h: h_.tensor_scalar(out=BM[:, h, 2, 256:384], in0=BM[:, h, 2, 256:384],
                                                            scalar1=cst[:, 223:224], scalar2=None, op0=ALU.add),
                     [('BM', h, 2), 'cst'], [('BM', h, 2)])
            act(sinkexp[:, 0:4], cst[:, l * CL + 98:l * CL + 102], AF.Exp, ['cst', 'negcA'], ['sinkexp'],
                bias=negcA[:, 0:1], scale=1.0)
            rk = ['qA', 'kA0', 'kA1', 'kA2']
            rv = ['vA0', 'vA1', 'vA2']
            units = [(h, i) for h in range(4) for i in range(16)]

            def score(ui):
                h, i = units[ui]
                kv = h // 2
                var = 1 if i == 0 else (2 if i == 15 else 0)
                bs = ui % 3
                l2 = ui % 2
                p3 = ui % 3
                for jj in range(3):
                    mm(P[bs][:, jj * 128:(jj + 1) * 128], kA[:, kv, (i + jj) * 128:(i + jj + 1) * 128],
                       qA[:, h, i * 128:(i + 1) * 128], True, True, rk, [('P', bs)])
                dve_stt(lg[l2], P[bs][:, 0:384], SCALE, BM[:, h, var, :], ALU.mult, ALU.add,
                        [('P', bs), ('BM', h, var)], [('lg', l2)])
                act(pT[p3], lg[l2], AF.Exp, [('lg', l2), 'negcA'], [('pT', p3)], bias=negcA[:, 0:1], scale=1.0)

            def finish(ui):
                h, i = units[ui]
                kv = h // 2
                bo = 3 + ui % 3
                l2 = ui % 2
                p3 = ui % 3
                for jj in range(3):
                    mm(P[bo][:, 0:128], vA[:, i + jj, kv * 128:(kv + 1) * 128], pT[p3][:, jj * 128:(jj + 1) * 128],
                       jj == 0, jj == 2, rv + [('pT', p3)], [('P', bo)])
                for jj in range(3):
                    mm(P[bo][:, 128:256], ones_bf[:, :], pT[p3][:, jj * 128:(jj + 1) * 128],
                       jj == 0, jj == 2, ['ones_bf', ('pT', p3)], [('P', bo)])
                S.op('dve', lambda h_, o=dd[l2], i_=P[bo][:, 128:256], s_=sinkexp[:, h:h + 1]:
                     h_.tensor_scalar(out=o, in0=i_, scalar1=s_, scalar2=None, op0=ALU.add),
                     [('P', bo), 'sinkexp'], [('dd', l2)])
                dve_recip(dd[l2], dd[l2], [('dd', l2)], [('dd', l2)])
                dve_tt(yA[:, h, i * 128:(i + 1) * 128], P[bo][:, 0:128], dd[l2], ALU.mult,
                       [('P', bo), ('dd', l2)], [('yA', h)])
                if i == 15:
                    dma('sp', SL('ya'), y_s[h], yA[:, h, :], [('yA', h)], [('y_s', h)])

            score(0)
            for ui in range(len(units)):
                if ui + 1 < len(units):
                    score(ui + 1)
                finish(ui)

        def attn_B(l):
            qB = xn_bf[:, 0:6 * NT].rearrange("p (h n) -> p h n", h=6)
            kB = AR_bf[:, 0:2 * 2 * NT].rearrange("p (h n) -> p h n", h=2)
            vB = AR_bf[:, 8192:8192 + 32 * 256].rearrange("p (b c) -> p b c", b=32)
            pT = [xT_bf[:, i * 512:(i + 1) * 512] for i in range(4)]
            rd = [xT_flat[:, 2048 + i * 512:2048 + (i + 1) * 512] for i in range(2)]
            yB = [xT_bf[:, 8192 + i * 512:8192 + (i + 1) * 512] for i in range(2)]
            B_ = Batch('ldB')
            B_.ld(qB, q_s[4:10].rearrange("h p n -> p h n"), 'qB')
            for s in range(2):
                B_.ld(kB[:, :, s * NT:(s + 1) * NT], KALL[s][:, 2:4, :], ('kB', s))
                for c8 in range(8):
                    B_.ld(vB[:, s * 16 + c8 * 2:s * 16 + c8 * 2 + 2, :],
                          VALL[s][c8, :, 256:512].rearrange("(b p) d -> p b d", p=128), ('vB', s, c8))
            B_.done()
            units = [(h, qt, kb) for h in range(6) for qt in range(4) for kb in range(32)]

            def score(ui):
                h, qt, kb = units[ui]
                kv = h // 3
                seg = kb // 16
                bs = ui % 4
                mm(P[bs][:, :], kB[:, kv, kb * 128:(kb + 1) * 128], qB[:, h, qt * 512:(qt + 1) * 512],
                   True, True, ['qB', ('kB', seg)], [('P', bs)])
                act(pT[bs], P[bs][:, :], AF.Exp, [('P', bs), 'negcB'], [('pT', bs)],
                    bias=negcB[:, seg:seg + 1], scale=SCALE)

            def finish(ui):
                h, qt, kb = units[ui]
                kv = h // 3
                seg = kb // 16
                p4 = ui % 4
                ti = ui // 32
                bo = 4 + 2 * (ti % 2)
                y2 = ti % 2
                mm(P[bo][:, :], vB[:, kb, kv * 128:(kv + 1) * 128], pT[p4], kb == 0, kb == 31,
                   [('vB', seg, (kb % 16) // 2), ('pT', p4)], [('P', bo)])
                mm(P[bo + 1][:, :], ones_bf[:, :], pT[p4], kb == 0, kb == 31,
                   ['ones_bf', ('pT', p4)], [('P', bo + 1)])
                if kb == 31:
                    dve_recip(rd[y2], P[bo + 1][:, :], [('P', bo + 1)], [('rd', y2)])
                    dve_tt(yB[y2], P[bo][:, :], rd[y2], ALU.mult, [('P', bo), ('rd', y2)], [('yB', y2)])
                    dma('sp', SL('yb%d' % y2), y_s[4 + h, :, qt * 512:(qt + 1) * 512], yB[y2], [('yB', y2)],
                        [('y_s', 4 + h, qt)])

            score(0)
            score(1)
            for ui in range(len(units)):
                if ui + 2 < len(units):
                    score(ui + 2)
                finish(ui)

        def attn_C(l):
            qC = xn_bf[:, 0:6 * NT].rearrange("p (h n) -> p h n", h=6)
            ACC = AR[:, 0:4 * NT].rearrange("p (a n) -> p a n", a=4)
            RS = [1, 4, 16]
            koff = []
            vboff = []
            o = 0
            ob = 0
            for r in RS:
                koff.append(o)
                o += NT + 128 * r
                vboff.append(ob)
                ob += 32 + 2 * r
            kC = xT_bf[:, 0:o]
            vC = AR_bf[:, 16384:16384 + ob * 128].rearrange("p (b c) -> p b c", b=ob)
            BMc = xT_flat[:, 4608:4608 + 3 * 3 * 384].rearrange("p (a v n) -> p a v n", a=3, v=3)
            lg = [xT_flat[:, 8192 + i * 384:8192 + (i + 1) * 384] for i in range(2)]
            pT = [xT_bf[:, 2 * 9216 + i * 384:2 * 9216 + (i + 1) * 384] for i in range(3)]
            yC = xT_bf[:, 2 * 10240:2 * 10240 + 2 * NT].rearrange("p (g n) -> p g n", g=2)
            rdc = xT_flat[:, 12288:12288 + NT]
            B0 = Batch('ldC')
            B1 = Batch('ldC2')
            B0.ld(qC, q_s[10:16].rearrange("h p n -> p h n"), 'qC')
            rkp = {0: ['qC'], 1: ['qC'], 2: ['qC']}
            rvp = {0: [], 1: [], 2: []}
            for p, r in enumerate(RS):
                B_ = B0 if p == 0 else B1
                rk = rkp[p]
                rv = rvp[p]
                hw = 64 * r
                ko = koff[p]
                krow = (4 + p) * 128
                B_.ld(kC[:, ko + hw:ko + hw + NT], kT_loc[krow:krow + 128, :], ('kC', p, 0))
                B_.ld(kC[:, ko:ko + hw], KALL[0][:, 4 + p, NT - hw:NT], ('kC', p, 1))
                B_.ld(kC[:, ko + hw + NT:ko + 2 * hw + NT], KALL[1][:, 4 + p, 0:hw], ('kC', p, 2))
                rk += [('kC', p, 0), ('kC', p, 1), ('kC', p, 2)]
                nb = 32 // r
                vc = v_loc[:, 512 + p * 128:512 + (p + 1) * 128]
                for c in range(r):
                    base = vboff[p] + c * (nb + 2)
                    src = vc.rearrange("(b i c) d -> c i b d", i=64, c=r)[c]
                    B_.ld(vC[0:64, base + 1:base + 1 + nb, :], src, ('vC', p, c, 0))
                    cs = slice(512 + p * 128, 512 + (p + 1) * 128)
                    npc = max(1, hw // 256)
                    plen = hw // npc
                    for pc in range(npc):
                        i0 = pc * plen // r
                        ni = plen // r
                        tk = NT - hw + pc * plen
                        src0 = VALL[0][tk // 256, tk % 256:tk % 256 + plen, cs].rearrange("(i c) d -> c i d", c=r)[c]
                        B_.ld(vC[i0:i0 + ni, base, :], src0, ('vC', p, c, 1, pc))
                        tk = pc * plen
                        src1 = VALL[1][tk // 256, tk % 256:tk % 256 + plen, cs].rearrange("(i c) d -> c i d", c=r)[c]
                        B_.ld(vC[i0:i0 + ni, base + 1 + nb, :], src1, ('vC', p, c, 2, pc))
                        rv += [('vC', p, c, 1, pc), ('vC', p, c, 2, pc)]
                    rv += [('vC', p, c, 0)]
                for v in range(3):
                    for g in range(2):
                        hh = 4 + 2 * p + g
                        B_.ld(BMc[0:64, p, v, g * 192:(g + 1) * 192],
                              bm_s[hh, OH_A + p * OH_C:OH_A + (p + 1) * OH_C].rearrange("(k n) -> k n", k=64),
                              ('BMc', p, v, g))
            B0.done()
            B1.done()
            for p, r in enumerate(RS):
                for g in range(2):
                    S.op('dve', lambda h_, o_=BMc[0:64, p, 1, g * 192:g * 192 + 64]: h_.tensor_scalar(
                        out=o_, in0=o_, scalar1=cst[0:64, 222:223], scalar2=None, op0=ALU.add),
                        [('BMc', p, 1, g), 'cst'], [('BMc', p, 1, g)])
                    S.op('dve', lambda h_, o_=BMc[0:64, p, 2, g * 192 + 128:g * 192 + 192]: h_.tensor_scalar(
                        out=o_, in0=o_, scalar1=cst[0:64, 223:224], scalar2=None, op0=ALU.add),
                        [('BMc', p, 2, g), 'cst'], [('BMc', p, 2, g)])
            units = [(p, r, c, mb) for p, r in enumerate(RS) for c in range(r) for mb in range(32 // r)]

            def score(ui):
                p, r, c, mb = units[ui]
                nb = 32 // r
                hw = 64 * r
                var = 1 if mb == 0 else (2 if mb == nb - 1 else 0)
                bs = ui % 3
                l2 = ui % 2
                p3 = ui % 3
                q0 = mb * 64 * r + c
                for g in range(2):
                    for jj in range(3):
                        k0 = koff[p] + hw + (mb - 1 + jj) * 64 * r + c
                        mm(P[bs][0:64, g * 192 + jj * 64:g * 192 + (jj + 1) * 64],
                           kC[:, k0:k0 + 63 * r + 1:r], qC[:, 2 * p + g, q0:q0 + 63 * r + 1:r], True, True, rkp[p],
                           [('P', bs)])
                dve_stt(lg[l2][0:64, :], P[bs][0:64, 0:384], SCALE, BMc[0:64, p, var, :], ALU.mult, ALU.add,
                        [('P', bs), ('BMc', p, var, 0), ('BMc', p, var, 1)], [('lg', l2)])
                act(pT[p3][0:64, :], lg[l2][0:64, :], AF.Exp, [('lg', l2), 'negcA'], [('pT', p3)],
                    bias=negcA[0:64, 0:1], scale=1.0)

            def finish(ui):
                p, r, c, mb = units[ui]
                nb = 32 // r
                base = vboff[p] + c * (nb + 2)
                bo = 3 + ui % 3
                p3 = ui % 3
                q0 = mb * 64 * r + c
                for g in range(2):
                    for jj in range(3):
                        mm(P[bo][:, g * 64:(g + 1) * 64], vC[0:64, base + mb + jj, :],
                           pT[p3][0:64, g * 192 + jj * 64:g * 192 + (jj + 1) * 64], jj == 0, jj == 2,
                           rvp[p] + [('pT', p3)], [('P', bo)])
                for g in range(2):
                    for jj in range(3):
                        mm(P[bo][:, 128 + g * 64:128 + (g + 1) * 64], ones_bf[0:64, :],
                           pT[p3][0:64, g * 192 + jj * 64:g * 192 + (jj + 1) * 64], jj == 0, jj == 2,
                           ['ones_bf', ('pT', p3)], [('P', bo)])
                accv = ACC[:, :, q0:q0 + 63 * r + 1:r]
                psv = P[bo][:, 0:256].rearrange("p (a n) -> p a n", a=4)
                if p == 0:
                    dve_copy(accv, psv, [('P', bo)], ['ACC'])
                else:
                    dve_tt(accv, accv, psv, ALU.add, [('P', bo), 'ACC'], ['ACC'])

            score(0)
            for ui in range(len(units)):
                if ui + 1 < len(units):
                    score(ui + 1)
                finish(ui)
            for g in range(2):
                dve_recip(rdc, ACC[:, 2 + g, :], ['ACC'], ['rdc'])
                dve_tt(yC[:, g, :], ACC[:, g, :], rdc, ALU.mult, ['ACC', 'rdc'], [('yC', g)], eng='pool')
                dma('sp', SL('yc'), y_s[10 + g], yC[:, g, :], [('yC', g)], [('y_s', 10 + g)])

        def merge_phase(l, tt):
            t0 = tt * T
            wgb = [carve(AR, i * 12288, 12288, BF16) for i in range(2)]
            wbrb = [carve(AR, 24576 + i * 3072, 3072, BF16) for i in range(2)]
            wob = [carve(AR, i * 4096, 4096, BF16) for i in range(4)]
            mg = carve(AR, 30720, 32768, BF16).rearrange("p (k n) -> p k n", k=KC)
            yT = carve(AR, 63488, 24576, BF16).rearrange("p (k n) -> p k n", k=12)
            gt = [sq[0][:, :], sq[1][:, :], sg[0][:, :]]
            tmp = [sg[1][:, :], rstd[:, 0:512], rstd[:, 512:1024]]
            dma('sp', SL('m1'), xn[:, :, 0:512], hn_s[:, :, t0:t0 + 512].rearrange("k p t -> p k t"), [],
                [('xn', k, 0) for k in range(KC)])
            dma('act', SL('m2'), yT[:, :, :], y_s[:, :, t0:t0 + T].rearrange("h p t -> p h t"), [], ['yT'])
            dma('sp', SL('m3'), xn[:, :, 512:T], hn_s[:, :, t0 + 512:t0 + T].rearrange("k p t -> p k t"), [],
                [('xn', k, 1) for k in range(KC)])
            def issue_w(m):
                s2 = m % 2
                for b in range(3):
                    dma('pool', SL('wg%d_%d' % (s2, b)), wgb[s2][:, b * 2048:(b + 1) * 2048],
                        wg_d[l * 48 + b * 16 + m], [], [('wg', s2, b)])
                dma('pool', SL('wbr%d' % s2), wbrb[s2], wbr_d[l * 16 + m], [], [('wbr', s2)])

            issue_w(0)
            for m in range(KC):
                s2 = m % 2
                if m + 1 < KC:
                    issue_w(m + 1)
                for half in range(2):
                    hs = slice(half * 512, (half + 1) * 512)
                    for b in range(3):
                        for k in range(KC):
                            mm(P[b][:, :], wgb[s2][:, b * 2048 + k * 128:b * 2048 + (k + 1) * 128], xn[:, k, hs],
                               k == 0, k == KC - 1, [('wg', s2, b), ('xn', k, half)], [('P', b)])
                        gc = l * CL + 48 + b * 16 + m
                        tsig = act(gt[b], P[b][:, :], AF.Sigmoid, [('P', b), 'cst'], [('gt', b)],
                                   bias=cst[:, gc:gc + 1], scale=1.0)
                        if m == 0 and half == 0 and b == 0:
                            S.dma('sp', SL('m0'), lambda h: h.dma_start(
                                out=xT[:, :, :], in_=xT_s[:, :, t0:t0 + T].rearrange("k p t -> p k t")), [],
                                [('xT', k_, h_) for k_ in range(KC) for h_ in range(2)], extra=[tsig])
                    for (b, k0, k1) in ((0, 0, 4), (1, 4, 10), (2, 10, 12)):
                        for kh in range(k0, k1):
                            mm(P[3 + b][:, :], wbrb[s2][:, kh * 128:(kh + 1) * 128], yT[:, kh, hs], kh == k0,
                               kh == k1 - 1, [('wbr', s2), 'yT'], [('P', 3 + b)])
                    for b in range(3):
                        dve_tt(tmp[b], gt[b], P[3 + b][:, :], ALU.mult, [('gt', b), ('P', 3 + b)], [('tmp', b)])
                    dve_tt(tmp[0], tmp[0], tmp[1], ALU.add, [('tmp', 0), ('tmp', 1)], [('tmp', 0)], eng='pool')
                    dve_tt(mg[:, m, hs], tmp[0], tmp[2], ALU.add, [('tmp', 0), ('tmp', 2)], [('mg', m, half)], eng='pool')
            S.barrier()
            for m in range(KC):
                s2 = m % 4
                dma('pool', SL('wo%d' % s2), wob[s2], wo_d[l * 16 + m], [], [('wo', s2)])
                for half in range(2):
                    hs = slice(half * 512, (half + 1) * 512)
                    b = 6 + (m * 2 + half) % 2
                    for k in range(KC):
                        mm(P[b][:, :], wob[s2][:, k * 128:(k + 1) * 128], mg[:, k, hs], k == 0, k == KC - 1,
                           [('wo', s2), ('mg', k, half)], [('P', b)])
                    dve_tt(xT[:, m, hs], P[b][:, :], xT[:, m, hs], ALU.add, [('P', b), ('xT', m, half)],
                           [('xT', m, half)])


        def final_out(tt):
            t0 = tt * T
            norm_stats()
            for k in range(KC):
                for half in range(2):
                    hs = slice(half * 512, (half + 1) * 512)
                    dve_stt(xT[:, k, hs], xT[:, k, hs], cst[:, 204 + k:205 + k], P[5 + half][:, :],
                            ALU.mult, ALU.mult, [('xT', k, half), ('P', 5 + half), 'cst'], [('xT', k, half)])
            ui = 0
            for tb in range(T // 128):
                s2 = tb % 2
                ost = carve(AR, s2 * 8192, 8192, F32)
                for k4 in range(4):
                    b = ui % 4
                    ui += 1
                    for kk in range(4):
                        k = k4 * 4 + kk
                        mm(P[b][:, kk * 128:(kk + 1) * 128], xT[:, k, tb * 128:(tb + 1) * 128], ident[:, :], True, True,
                           [('xT', k, tb // 4), 'ident'], [('P', b)])
                    if k4 % 2 == 0:
                        dve_copy(ost[:, k4 * 512:(k4 + 1) * 512], P[b][:, :], [('P', b)], [('ost', s2, k4)])
                    else:
                        act(ost[:, k4 * 512:(k4 + 1) * 512], P[b][:, :], AF.Copy, [('P', b)], [('ost', s2, k4)])
                r0 = t0 + tb * 128
                dma('sp', SL('ost%d' % s2), y_d[r0:r0 + 128, :], ost, [('ost', s2, k4) for k4 in range(4)],
                    [('y', tt, tb)])

        class _Stop(Exception):
            pass

        stage = [0]

        def step(fn, *a):
            stage[0] += 1
            if DEBUG_STAGE is not None and stage[0] > DEBUG_STAGE:
                raise _Stop()
            fn(*a)
            S.barrier()

        try:
            for tt in range(NTT):
                step(load_x_tile, tt)
                step(ffn, 0, 0)
                step(mix_in, 0, tt)
            for l in range(L):
                if l == 0:
                    exchange()
                    step(bm_compute)
                else:
                    step(exchange)
                step(attn_A, l)
                step(attn_B, l)
                step(attn_C, l)
                for tt in range(NTT):
                    step(merge_phase, l, tt)
                    if l + 1 < L:
                        ffn(l, 1)
                        step(ffn, l + 1, 0)
                    else:
                        step(ffn, l, 1)
                    if l + 1 < L:
                        step(mix_in, l + 1, tt)
                    else:
                        step(final_out, tt)
        except _Stop:
            pass
        S.finish(block)
    return nc


def _t5_bucket_np(rel):
    half = 16
    me = 8
    ret = np.where(rel > 0, half, 0)
    n = np.abs(rel)
    nf = np.maximum(n, 1).astype(np.float32)
    large = me + (np.log(nf / np.float32(me)) / np.float32(math.log(2048 / me)) * np.float32(half - me)).astype(np.int32)
    large = np.minimum(large, half - 1)
    return ret + np.where(n < me, n, large)


def _onehot_tables():
    oh = np.zeros((33, OH_N), np.float32)

    def fill(off, blk, dil):
        kl = np.arange(blk)[:, None]
        n = np.arange(3 * blk)[None, :]
        jj = n // blk
        ql = n % blk
        rel = (jj - 1) * blk + kl - ql
        valid = np.abs(rel) <= blk
        bk = _t5_bucket_np((rel * dil).astype(np.int32))
        cols = off + (kl * 3 * blk + n)
        for b in range(32):
            sel = valid & (bk == b)
            oh[b, cols[sel]] = 1.0
        oh[32, cols[~valid]] = 1.0

    fill(0, 128, 1)
    for p, dil in enumerate((1, 4, 16)):
        fill(OH_A + p * OH_C, 64, dil)
    return oh


def _rope_tables(base):
    pos = base + np.arange(NT)
    row = (pos // 64).astype(np.float32)
    col = (pos % 64).astype(np.float32)
    inv = (np.float32(10000.0) ** (-(np.arange(0, 64, 2, dtype=np.float32) / np.float32(64)))).astype(np.float32)
    ar = (row[:, None] * inv[None, :]).astype(np.float32)
    ac = (col[:, None] * inv[None, :]).astype(np.float32)
    C = np.zeros((128, NT), np.float32)
    Sg = np.zeros((128, NT), np.float32)
    for a0, ang in ((0, ar), (64, ac)):
        c = np.cos(ang).astype(np.float32).T
        s = np.sin(ang).astype(np.float32).T
        C[a0:a0 + 32] = c
        C[a0 + 32:a0 + 64] = c
        Sg[a0:a0 + 32] = -s
        Sg[a0 + 32:a0 + 64] = s
    return C, Sg


_NC_CACHE = {}


def kernel(x_prompt, x_sample, ffn1_norm, ffn1_w13, ffn1_w2, mix_norm, w_in, q_gain_b, k_gain_b,
           sink_a, w_gate, b_gate, w_br_a, w_br_b, w_br_c, w_o, ffn2_norm, ffn2_w13, ffn2_w2,
           rel_bias, final_norm):
    f32 = np.float32
    A = lambda a: np.ascontiguousarray(np.asarray(a, dtype=f32))
    shared = {}

    def w13_layout(w):
        a = np.asarray(w, f32).reshape(L, 16, 128, 2, FC, 128).transpose(0, 4, 2, 1, 3, 5)
        return np.ascontiguousarray(a).reshape(L * FC, 128, 4096)

    shared["w13_1"] = w13_layout(ffn1_w13)
    shared["w13_2"] = w13_layout(ffn2_w13)
    shared["w2_1"] = A(ffn1_w2).reshape(L * FC, 128, D)
    shared["w2_2"] = A(ffn2_w2).reshape(L * FC, 128, D)
    win = np.asarray(w_in, f32)
    qk_starts = ([0, 128, 256, 384, 512, 640] + [1024 + 128 * i for i in range(6)] + [1792, 1920] +
                 [2304 + 128 * i for i in range(6)] + [3072, 3200, 3328])
    qk_cols = np.concatenate([np.arange(s, s + 128) for s in qk_starts])
    a = win[:, :, qk_cols].reshape(L, 16, 128, 23, 128).transpose(0, 3, 2, 1, 4)
    shared["wqk"] = np.ascontiguousarray(a).reshape(L * 23, 128, 2048)
    v_cols = np.concatenate([np.arange(768, 1024), np.arange(2048, 2304), np.arange(3456, 3840)])
    a = win[:, :, v_cols].reshape(L, 16, 128, 896).transpose(0, 2, 1, 3)
    shared["wv"] = np.ascontiguousarray(a).reshape(L, 128, 16 * 896)
    a = np.asarray(w_gate, f32).reshape(L, 16, 128, 48, 128).transpose(0, 3, 2, 1, 4)
    shared["wg"] = np.ascontiguousarray(a).reshape(L * 48, 128, 2048)
    wbr = np.concatenate([np.asarray(w_br_a, f32), np.asarray(w_br_b, f32), np.asarray(w_br_c, f32)], axis=1)
    a = wbr.reshape(L, 12, 128, 16, 128).transpose(0, 3, 2, 1, 4)
    shared["wbr"] = np.ascontiguousarray(a).reshape(L * 16, 128, 1536)
    a = np.asarray(w_o, f32).reshape(L, 16, 128, 16, 128).transpose(0, 3, 2, 1, 4)
    shared["wo"] = np.ascontiguousarray(a).reshape(L * 16, 128, 2048)
    shared["oh"] = _onehot_tables()
    shared["rbx"] = np.concatenate([np.asarray(rel_bias, f32), np.full((1, 10), NEG, f32)], axis=0)
    pm = np.zeros((128, 128), f32)
    for m in range(128):
        pm[(m + 32) if (m % 64) < 32 else (m - 32), m] = 1.0
    shared["perm"] = pm
    shared["ident"] = np.eye(128, dtype=f32)

    cst0 = np.zeros((128, NCST), f32)
    for l in range(L):
        b = l * CL
        cst0[:, b:b + 16] = np.asarray(ffn1_norm, f32)[l].reshape(16, 128).T
        cst0[:, b + 16:b + 32] = np.asarray(mix_norm, f32)[l].reshape(16, 128).T
        cst0[:, b + 32:b + 48] = np.asarray(ffn2_norm, f32)[l].reshape(16, 128).T
        cst0[:, b + 48:b + 96] = np.asarray(b_gate, f32)[l].reshape(48, 128).T
        cst0[:, b + 96] = np.asarray(q_gain_b, f32)[l]
        cst0[:, b + 97] = np.asarray(k_gain_b, f32)[l]
        cst0[:, b + 98:b + 102] = np.asarray(sink_a, f32)[l][None, :]
    cst0[:, 204:220] = np.asarray(final_norm, f32).reshape(16, 128).T

    xp = np.asarray(x_prompt, f32)
    xs = np.asarray(x_sample, f32)
    in_maps = []
    ropes = {0: _rope_tables(0), NT: _rope_tables(NT)}
    for c in range(8):
        m = dict(shared)
        cst = cst0.copy()
        if c < 4:
            m["x"] = np.ascontiguousarray(xp[c])
            base = 0
            rank = c % 2
            flags = [0.0 if rank == 0 else NEG, 0.0 if rank == 1 else NEG, NEG, NEG]
        else:
            sidx = (c - 4) // 2
            rank = c % 2
            m["x"] = np.ascontiguousarray(xs[sidx, rank * NT:(rank + 1) * NT])
            base = rank * NT
            flags = [0.0, 0.0, 0.0 if rank == 1 else NEG, 0.0 if rank == 0 else NEG]
        cst[:, 220:224] = np.asarray(flags, f32)[None, :]
        m["cst"] = cst
        m["rope_c"], m["rope_s"] = ropes[base]
        in_maps.append(m)

    if "nc" not in _NC_CACHE:
        _NC_CACHE["nc"] = build()
    nc = _NC_CACHE["nc"]
    res = run_bass_kernel_spmd(nc, in_maps, core_ids=list(range(8)))
    ys = [np.asarray(res.results[c]["y"], dtype=f32) for c in range(8)]
    y_prompt = np.stack(ys[0:4], axis=0)
    y_sample = np.stack([np.concatenate(ys[4:6], axis=0), np.concatenate(ys[6:8], axis=0)], axis=0)
    return (y_prompt, y_sample)
```
